# Optimizing a Trainium2 kernel written in Bass

```python
import math
import jax, jax.numpy as jnp
from jax import lax
import numpy as np

D_MODEL = 1024
BATCH = 8
SEQ = 4096
DEPTH = 1

N_HEADS = 8
N_KV_HEADS = 2
HEAD_DIM = 64
GQA_GROUP = N_HEADS // N_KV_HEADS
ATTN_WIDTH = N_HEADS * HEAD_DIM
KV_WIDTH = N_KV_HEADS * HEAD_DIM
ROPE_DIM = HEAD_DIM // 4
ROPE_THETA = 500000.0
WINDOW = 128
BLOCK = 128
SSM_GROUP_CH = 16
SSM_WIDTH = D_MODEL // 2
SSM_GROUPS = SSM_WIDTH // SSM_GROUP_CH
SSM_STATE = 64
D_FF = -(-8 * D_MODEL // (3 * 256)) * 256
IN_COLS = ATTN_WIDTH + 2 * KV_WIDTH + SSM_WIDTH + 2 * D_MODEL
SPLITS = [ATTN_WIDTH, ATTN_WIDTH + KV_WIDTH, ATTN_WIDTH + 2 * KV_WIDTH,
          ATTN_WIDTH + 2 * KV_WIDTH + SSM_WIDTH,
          ATTN_WIDTH + 2 * KV_WIDTH + SSM_WIDTH + D_MODEL]
RMS_EPS = 1e-6
NEG_INF = -1e30

kernel_name = "hybrid_gated_swa_s5_encoder"


def rmsnorm(x, g):
    xf = x.astype(jnp.float32)
    xf = xf * lax.rsqrt(jnp.mean(xf * xf, axis=-1, keepdims=True) + RMS_EPS)
    return (xf * g.astype(jnp.float32)).astype(x.dtype)


def rope_tables(seq_len):
    pos = jnp.arange(seq_len, dtype=jnp.float32)
    inv_freq = ROPE_THETA ** (-jnp.arange(0, ROPE_DIM, 2, dtype=jnp.float32) / ROPE_DIM)
    ang = pos[:, None] * inv_freq[None, :]
    return jnp.cos(ang)[:, None, :], jnp.sin(ang)[:, None, :]


def partial_rope(t, cos, sin):
    t = t.astype(jnp.float32)
    r, rest = t[..., :ROPE_DIM], t[..., ROPE_DIM:]
    r1, r2 = r[..., :ROPE_DIM // 2], r[..., ROPE_DIM // 2:]
    rot = jnp.concatenate([r1 * cos - r2 * sin, r2 * cos + r1 * sin], axis=-1)
    return jnp.concatenate([rot, rest], axis=-1)


def windowed_gqa(q, k, v, sink):
    b, L = q.shape[0], q.shape[1]
    nb = L // BLOCK
    qb = q.reshape(b, nb, BLOCK, N_KV_HEADS, GQA_GROUP, HEAD_DIM)
    pad = ((0, 0), (1, 1), (0, 0), (0, 0), (0, 0))
    kp = jnp.pad(k.reshape(b, nb, BLOCK, N_KV_HEADS, HEAD_DIM), pad)
    vp = jnp.pad(v.astype(jnp.float32).reshape(b, nb, BLOCK, N_KV_HEADS, HEAD_DIM), pad)
    kw = jnp.concatenate([kp[:, :-2], kp[:, 1:-1], kp[:, 2:]], axis=2)
    vw = jnp.concatenate([vp[:, :-2], vp[:, 1:-1], vp[:, 2:]], axis=2)
    scores = jnp.einsum('bnqkgd,bnskd->bnkgqs', qb, kw) * (HEAD_DIM ** -0.5)
    blk = jnp.arange(nb)[:, None, None]
    qpos = blk * BLOCK + jnp.arange(BLOCK)[None, :, None]
    kpos = (blk - 1) * BLOCK + jnp.arange(3 * BLOCK)[None, None, :]
    valid = (jnp.abs(qpos - kpos) <= WINDOW) & (kpos >= 0) & (kpos < L)
    scores = jnp.where(valid[None, :, None, None], scores, NEG_INF)
    s = sink.astype(jnp.float32).reshape(N_KV_HEADS, GQA_GROUP)[None, None, :, :, None, None]
    m = jnp.maximum(jnp.max(scores, axis=-1, keepdims=True), s)
    p = jnp.exp(scores - m)
    p = p / (jnp.sum(p, axis=-1, keepdims=True) + jnp.exp(s - m))
    out = jnp.einsum('bnkgqs,bnskd->bnqkgd', p, vw)
    return out.reshape(b, L, ATTN_WIDTH)


def s5_scan(u, lam_re, lam_im, log_dt, b_re, b_im):
    L = u.shape[1]
    lam = lax.complex(lam_re.astype(jnp.float32), lam_im.astype(jnp.float32))
    dt = jnp.exp(log_dt.astype(jnp.float32))[:, None]
    lam_bar = jnp.exp(lam * dt)
    b_bar = ((lam_bar - 1.0) / lam)[..., None] * lax.complex(b_re.astype(jnp.float32),
                                                             b_im.astype(jnp.float32))
    bu = jnp.einsum('blgh,gph->blgp', u.astype(jnp.complex64), b_bar)
    a = jnp.broadcast_to(lam_bar, (1, L) + lam_bar.shape)

    def combine(c1, c2):
        a1, x1 = c1
        a2, x2 = c2
        return a1 * a2, a2 * x1 + x2

    _, h = lax.associative_scan(combine, (a, bu), axis=1)
    return h


def bidirectional_s5(u, lam_re, lam_im, log_dt, b_re, b_im, c_re, c_im, d, w_glu):
    b, L = u.shape[0], u.shape[1]
    ug = u.astype(jnp.float32).reshape(b, L, SSM_GROUPS, SSM_GROUP_CH)
    h_f = s5_scan(ug, lam_re[0], lam_im[0], log_dt[0], b_re[0], b_im[0])
    h_b = s5_scan(ug[:, ::-1], lam_re[1], lam_im[1], log_dt[1], b_re[1], b_im[1])[:, ::-1]
    c = lax.complex(c_re.astype(jnp.float32), c_im.astype(jnp.float32))
    y = jnp.einsum('blgp,ghp->blgh', h_f + h_b, c).real + d.astype(jnp.float32) * ug
    y = jax.nn.gelu(y.reshape(b, L, SSM_WIDTH)).astype(u.dtype)
    return y * jax.nn.sigmoid(y @ w_glu)


def setup_inputs(seed: int = 0) -> dict:
    key = jax.random.key(seed)
    ks = jax.random.split(key, 24)
    nrm = lambda k, shape, scale: jax.random.normal(k, shape, jnp.float32) * scale
    Ls, G, P, H = DEPTH, SSM_GROUPS, SSM_STATE, SSM_GROUP_CH
    lam_im_init = jnp.pi * jnp.arange(P, dtype=jnp.float32)
    return {
        "x": jax.random.normal(ks[0], (BATCH, SEQ, D_MODEL), jnp.float32),
        "norm1_g": 1.0 + nrm(ks[1], (Ls, D_MODEL), 0.02),
        "w_in": nrm(ks[2], (Ls, D_MODEL, IN_COLS), D_MODEL ** -0.5),
        "attn_sink": nrm(ks[3], (Ls, N_HEADS), 0.5),
        "ssm_lambda_re": -0.5 + nrm(ks[4], (Ls, 2, G, P), 0.01),
        "ssm_lambda_im": lam_im_init + nrm(ks[5], (Ls, 2, G, P), 0.01),
        "ssm_log_dt": jax.random.uniform(ks[6], (Ls, 2, G), jnp.float32,
                                         math.log(1e-3), math.log(1e-1)),
        "ssm_b_re": nrm(ks[7], (Ls, 2, G, P, H), (2.0 * H) ** -0.5),
        "ssm_b_im": nrm(ks[8], (Ls, 2, G, P, H), (2.0 * H) ** -0.5),
        "ssm_c_re": nrm(ks[9], (Ls, G, H, P), (2.0 * P) ** -0.5),
        "ssm_c_im": nrm(ks[10], (Ls, G, H, P), (2.0 * P) ** -0.5),
        "ssm_d": nrm(ks[11], (Ls, G, H), 1.0),
        "w_glu": nrm(ks[12], (Ls, SSM_WIDTH, SSM_WIDTH), SSM_WIDTH ** -0.5),
        "w_attn_branch": nrm(ks[13], (Ls, ATTN_WIDTH, D_MODEL), ATTN_WIDTH ** -0.5),
        "w_ssm_branch": nrm(ks[14], (Ls, SSM_WIDTH, D_MODEL), SSM_WIDTH ** -0.5),
        "w_out": nrm(ks[15], (Ls, D_MODEL, D_MODEL), D_MODEL ** -0.5),
        "norm2_g": 1.0 + nrm(ks[16], (Ls, D_MODEL), 0.02),
        "w_ffn_gate": nrm(ks[17], (Ls, D_MODEL, D_FF), D_MODEL ** -0.5),
        "w_ffn_up": nrm(ks[18], (Ls, D_MODEL, D_FF), D_MODEL ** -0.5),
        "w_ffn_down": nrm(ks[19], (Ls, D_FF, D_MODEL), D_FF ** -0.5),
        "norm_f_g": 1.0 + nrm(ks[20], (D_MODEL,), 0.02),
    }


def reference(x, norm1_g, w_in, attn_sink, ssm_lambda_re, ssm_lambda_im, ssm_log_dt,
              ssm_b_re, ssm_b_im, ssm_c_re, ssm_c_im, ssm_d, w_glu, w_attn_branch,
              w_ssm_branch, w_out, norm2_g, w_ffn_gate, w_ffn_up, w_ffn_down, norm_f_g):
    b, L, _ = x.shape
    cos, sin = rope_tables(L)
    for layer in range(DEPTH):
        h = rmsnorm(x, norm1_g[layer])
        proj = h @ w_in[layer]
        q, k, v, u, g_attn, g_ssm = jnp.split(proj, SPLITS, axis=-1)
        q = partial_rope(q.reshape(b, L, N_HEADS, HEAD_DIM), cos, sin)
        k = partial_rope(k.reshape(b, L, N_KV_HEADS, HEAD_DIM), cos, sin)
        v = v.reshape(b, L, N_KV_HEADS, HEAD_DIM)
        attn = windowed_gqa(q, k, v, attn_sink[layer]).astype(x.dtype)
        ssm = bidirectional_s5(u, ssm_lambda_re[layer], ssm_lambda_im[layer], ssm_log_dt[layer],
                               ssm_b_re[layer], ssm_b_im[layer], ssm_c_re[layer],
                               ssm_c_im[layer], ssm_d[layer], w_glu[layer])
        merged = (jax.nn.sigmoid(g_attn) * (attn @ w_attn_branch[layer])
                  + jax.nn.sigmoid(g_ssm) * (ssm @ w_ssm_branch[layer]))
        x = x + merged @ w_out[layer]
        h2 = rmsnorm(x, norm2_g[layer])
        x = x + (jax.nn.silu(h2 @ w_ffn_gate[layer]) * (h2 @ w_ffn_up[layer])) @ w_ffn_down[layer]
    return rmsnorm(x, norm_f_g)
```

```python
import math
import os
from contextlib import ExitStack
import numpy as np
import ml_dtypes
import concourse.bass as bass
import concourse.mybir as mybir
from concourse.bass_utils import run_bass_kernel_spmd

F32 = mybir.dt.float32
BF16 = mybir.dt.bfloat16
I32 = mybir.dt.int32
ALU = mybir.AluOpType
AF = mybir.ActivationFunctionType

L = 4096
D = 1024
NT = 32
DFF = 2816
NMF = 22
EPS = 1e-6
TWO_PI_S = float(2 * math.pi * (1 - 2e-6))
HALF_PI = float(math.pi / 2)


class Prog:
    ENG = ("pe", "act", "dve", "pool", "sp")

    def __init__(self, nc, stack):
        self.nc = nc
        self.stack = stack
        self.ops = []
        self.last_w = {}
        self.readers = {}
        self.sems = {}
        self.barrier_idx = None
        self.last_eng = {}
        self.dma_since = []

    def eng(self, name):
        nc = self.nc
        return {"pe": nc.tensor, "act": nc.scalar, "dve": nc.vector,
                "pool": nc.gpsimd, "sp": nc.sync}[name]

    def sem(self, key):
        if key not in self.sems:
            nm = "s_" + "_".join(str(x) for x in key)
            self.sems[key] = self.stack.enter_context(self.nc.semaphore(nm))
        return self.sems[key]

    def op(self, engine, fn, reads=(), writes=(), semkey=None, extra_deps=()):
        if getattr(self, "disabled", False):
            return -1
        idx = len(self.ops)
        if any(isinstance(k, tuple) and k and k[0] == "pb" for k in reads):
            writes = list(writes) + [k for k in reads if isinstance(k, tuple) and k and k[0] == "pb"]
            reads = [k for k in reads if not (isinstance(k, tuple) and k and k[0] == "pb")]
        deps = set(extra_deps)
        if self.barrier_idx is not None:
            deps.add(self.barrier_idx)
        for k in reads:
            w = self.last_w.get(k)
            if w is not None:
                deps.add(w)
        for k in writes:
            w = self.last_w.get(k)
            if w is not None:
                deps.add(w)
            for r in self.readers.get(k, ()):
                deps.add(r)
        deps.discard(idx)
        self.ops.append(dict(engine=engine, fn=fn, deps=deps, dma=semkey is not None,
                             semkey=semkey, signal=False))
        for k in reads:
            self.readers.setdefault(k, []).append(idx)
        for k in writes:
            self.last_w[k] = idx
            self.readers[k] = []
        if semkey is None:
            self.last_eng[engine] = idx
        else:
            self.dma_since.append(idx)
        return idx

    def dma(self, fn, reads=(), writes=(), semkey=None, queue="sp"):
        return self.op(queue, fn, reads, writes, semkey=semkey)

    def barrier(self):
        if getattr(self, "disabled", False):
            return
        deps = set(self.last_eng.values()) | set(self.dma_since)
        nc = self.nc
        old = self.barrier_idx
        self.barrier_idx = None
        idx = self.op("pool", lambda: nc.gpsimd.nop(), extra_deps=deps | ({old} if old is not None else set()))
        self.barrier_idx = idx
        self.dma_since = []
        self.last_w = {}
        self.readers = {}

    def emit(self, final_engine="sp"):
        ops = self.ops
        for i, o in enumerate(ops):
            nd = set()
            for d in o["deps"]:
                do = ops[d]
                if (not do["dma"]) and (not o["dma"]) and do["engine"] == o["engine"] == "pe":
                    continue
                nd.add(d)
            o["deps"] = nd
            for d in nd:
                ops[d]["signal"] = True
        for o in ops:
            if o["dma"]:
                o["signal"] = True
        cnt = {}
        for o in ops:
            if not o["signal"]:
                continue
            if o["dma"]:
                key = ("dma", o["semkey"])
                cnt[key] = cnt.get(key, 0) + 16
            else:
                key = ("eng", o["engine"])
                cnt[key] = cnt.get(key, 0) + 1
            o["sem"] = key
            o["val"] = cnt[key]
        waited = {}
        nwait = 0
        for i, o in enumerate(ops):
            e = self.eng(o["engine"])
            need = {}
            for d in o["deps"]:
                do = ops[d]
                s = do["sem"]
                need[s] = max(need.get(s, 0), do["val"])
            for s, v in need.items():
                wk = (o["engine"], s)
                if waited.get(wk, 0) >= v:
                    continue
                waited[wk] = v
                e.wait_ge(self.sem(s), v)
                nwait += 1
            ins = o["fn"]()
            if o["signal"]:
                ins.then_inc(self.sem(o["sem"]), 16 if o["dma"] else 1)
        fe = self.eng(final_engine)
        for key, v in cnt.items():
            if key[0] == "dma":
                fe.wait_ge(self.sem(key), v)
        return dict(n_ops=len(ops), n_wait=nwait, n_sems=len(self.sems))


CST = {}
_o = 0
for _n, _w in (("ident", 128), ("ropec", 256), ("ropes", 256), ("smf", 128), ("smb", 128), ("iota", 512),
               ("kvals", 16), ("sgn1", 1), ("sgn2", 1), ("g1T", 8), ("g2T", 8), ("sink", 8), ("dcol", 32),
               ("lamr", 64), ("lami", 64), ("ldt", 64)):
    CST[_n] = (_o, _w)
    _o += _w
NCST = _o
NCSTB = 128 + 1024


def build(debug=False):
    nc = bass.Bass("TRN2", target_bir_lowering=False)
    dt_in = lambda name, shape, dt=F32: nc.dram_tensor(name, shape, dt, kind="ExternalInput").ap()
    x_d = dt_in("x", [L, D])
    win_d = dt_in("w_in", [D, 3328])
    wglu_d = dt_in("w_glu", [512, 512])
    wab_d = dt_in("w_ab", [512, 1024])
    wsb_d = dt_in("w_sb", [512, 1024])
    wout_d = dt_in("w_out", [D, D])
    wfg_d = dt_in("w_fg", [D, DFF])
    wfu_d = dt_in("w_fu", [D, DFF])
    wfd_d = dt_in("w_fd", [DFF, D])
    gf_d = dt_in("gfb", [128, D])
    cst_d = dt_in("cst", [128, NCST])
    cstb_d = dt_in("cstb", [128, NCSTB], BF16)
    B1_d = dt_in("B1", [128, 1024])
    B2_d = dt_in("B2", [128, 1024])
    CT1_d = dt_in("CT1", [128, 512])
    CT2_d = dt_in("CT2", [128, 512])
    jp_d = dt_in("jperm", [128, 128], BF16)
    y_d = nc.dram_tensor("y", [L, D], F32, kind="ExternalOutput").ap()
    x2_d = nc.dram_tensor("x2s", [L, D], F32, kind="ExternalOutput" if debug else "Internal").ap()
    if debug:
        dbg_attn = nc.dram_tensor("dbg_attn", [128, 4 * L], BF16, kind="ExternalOutput").ap()
        dbg_ssm = nc.dram_tensor("dbg_ssm", [128, 4 * L], BF16, kind="ExternalOutput").ap()
        dbg_y = nc.dram_tensor("dbg_y", [128, 4 * 8 * 512], BF16, kind="ExternalOutput").ap()

    st = ExitStack()
    with st:
        ARENA_B = 208896
        arena = st.enter_context(nc.sbuf_tensor("arena", [128, ARENA_B // 2], BF16))
        psum = st.enter_context(nc.psum_tensor("psum", [128, 8, 512], F32))
        P = Prog(nc, st)

        def V(off, shape, dt=BF16, parts=None):
            n = int(np.prod(shape))
            esz = 2 if dt == BF16 else 4
            assert off % 4 == 0
            a = arena[:, off // 2:(off + n * esz) // 2]
            if dt != BF16:
                a = a.bitcast(dt)
            if len(shape) > 1:
                names = "abcd"[:len(shape)]
                kw = {names[i]: shape[i] for i in range(len(shape))}
                a = a.rearrange("p (" + " ".join(names) + ") -> p " + " ".join(names), **kw)
            return a

        def bank(i):
            return psum[:, i, :]

        def bankb(i):
            return psum[:, i, :].bitcast(BF16)

        P_OFF = 0
        XS_OFF = 14336
        STG_OFF = XS_OFF + 10240
        R1 = STG_OFF + 8192
        R2 = R1 + 32768
        R3 = R2 + 65536
        R4 = R3 + 32768
        assert R4 + 45056 == ARENA_B

        cst = V(P_OFF, [NCST], F32)
        assert NCST * 4 <= 6720
        po = P_OFF + 6720
        cstb = V(po, [NCSTB], BF16); po += NCSTB * 2
        ones_b = V(po, [128], BF16); po += 256
        es_t = V(po, [1024], BF16); po += 2048
        ssb = V(po, [96], F32); po += 384
        rsb = V(po, [96], F32); po += 384
        rstd = V(po, [96], F32); po += 384
        sm64 = V(po, [64 * 6], F32); po += 64 * 6 * 4
        esf = V(po, [8], F32); po += 64
        esh = V(po, [8], BF16); po += 64
        esl = V(po, [8], F32); po += 64
        assert po <= XS_OFF, po

        def C(name):
            o, w = CST[name]
            return cst[:, o:o + w]

        ident_f = C("ident")
        ident_b = cstb[:, 0:128]
        maskb = cstb[:, 128:1152]
        xs = [V(XS_OFF, [1024], F32), V(XS_OFF + 4096, [1024], F32)]
        xn = V(XS_OFF + 8192, [1024], BF16)
        stg = [V(STG_OFF, [1024], F32), V(STG_OFF + 4096, [1024], F32)]

        A = nc.scalar
        DV = nc.vector
        PL = nc.gpsimd
        PE = nc.tensor

        def mm(out, lhsT, rhs, start, stop):
            return lambda: PE.matmul(out, lhsT=lhsT, rhs=rhs, start=start, stop=stop)

        def tr(out, in_, ident):
            return lambda: PE.transpose(out, in_, ident)

        def act(out, in_, func, scale=1.0, bias=None, accum=None):
            kw = {}
            if bias is not None:
                kw["bias"] = bias
            if accum is not None:
                kw["accum_out"] = accum
            return lambda: A.activation(out=out, in_=in_, func=func, scale=scale, **kw)

        def tt(eng, out, in0, in1, op):
            return lambda: eng.tensor_tensor(out=out, in0=in0, in1=in1, op=op)

        def ts(eng, out, in0, s1, s2, op0, op1=None):
            if op1 is None:
                return lambda: eng.tensor_scalar(out=out, in0=in0, scalar1=s1, scalar2=None, op0=op0)
            return lambda: eng.tensor_scalar(out=out, in0=in0, scalar1=s1, scalar2=s2, op0=op0, op1=op1)

        def stt(out, in0, scalar, in1, op0, op1):
            return lambda: DV.scalar_tensor_tensor(out=out, in0=in0, scalar=scalar, in1=in1, op0=op0, op1=op1)

        def cp(eng, out, in_):
            if eng is A:
                return lambda: A.copy(out=out, in_=in_)
            return lambda: eng.tensor_copy(out=out, in_=in_)

        def dma(out, in_):
            return lambda: nc.sync.dma_start(out=out, in_=in_)

        stg_n = [0]

        WCOLS = {}

        def load_w(dram, dst, ncols, nk, key, per_chunk_sem=False, per_piece_sem=False, pieces=None):
            WCOLS[key] = ncols
            if pieces is None:
                pieces = [(kc, c0) for kc in range(nk) for c0 in range(0, ncols, 1024)]
            for (kc, c0) in pieces:
                w = min(1024, ncols - c0)
                if per_piece_sem:
                    sk = ("%s_c%d" % (key, c0)) if ncols > 1024 else ("%s_g%d" % (key, kc // 6))
                elif per_chunk_sem:
                    sk = key + str(kc)
                else:
                    sk = key
                P.dma((lambda kc=kc, c0=c0, w=w: PL.dma_start(out=dst[:, kc, c0:c0 + w],
                                                              in_=dram[kc * 128:(kc + 1) * 128, c0:c0 + w])),
                      writes=[(key, kc, c0)], semkey=sk, queue="pool")

        def wk_all(key, nk):
            return [(key, kc, c0) for kc in range(nk) for c0 in range(0, WCOLS[key], 1024)]

        sscol = [0]

        def norm_tile(src_dram, slot, hT_dst, pbank, xkey, gT):
            col = sscol[0] % 96
            sscol[0] += 1
            P.dma(dma(xs[slot][:], src_dram), writes=[("xs", slot)], semkey="xs%d" % slot)
            P.op("act", act(xn[:], xs[slot][:], AF.Square, accum=ssb[:, col:col + 1]),
                 reads=[("xs", slot)], writes=["xn", ("ss", col)])
            P.op("act", act(rsb[:, col:col + 1], ssb[:, col:col + 1], AF.Sqrt, scale=1.0 / D, bias=EPS),
                 reads=[("ss", col)], writes=[("rs", col)])
            P.op("dve", lambda: DV.reciprocal(out=rstd[:, col:col + 1], in_=rsb[:, col:col + 1]),
                 reads=[("rs", col)], writes=[("rstd", col)])
            P.op("dve", ts(DV, xn[:], xs[slot][:], rstd[:, col:col + 1], None, ALU.mult),
                 reads=[("xs", slot), ("rstd", col)], writes=["xn"])
            pb = bankb(pbank)
            for k in range(8):
                P.op("pe", tr(pb[:, k * 128:(k + 1) * 128], xn[:, k * 128:(k + 1) * 128], ident_b),
                     reads=["xn", "cstb"], writes=[("pb", pbank)])
            P.op("dve", tt(DV, hT_dst, pb.rearrange("p (k c) -> p k c", k=8),
                           gT.unsqueeze(2).to_broadcast([128, 8, 128]), ALU.mult),
                 reads=[("pb", pbank), "cst"], writes=[xkey])
            return col

        PHASE = [0]
        KSTOP = int(os.environ.get('KSTOP', '99'))
        P.dma(dma(cst[:], cst_d), writes=["cst"], semkey="c0")
        P.dma(dma(cstb[:], cstb_d), writes=["cstb"], semkey="c1")
        P.op("pool", lambda: PL.memset(ones_b[:], 1.0), writes=["ones"])
        P.op("pool", lambda: PL.memset(es_t[:], 0.0), writes=["es"])
        P.op("act", act(esf[:], C("sink"), AF.Exp), reads=["cst"], writes=["esf"])
        P.op("dve", cp(DV, esh[:], esf[:]), reads=["esf"], writes=["esh"])
        P.op("dve", tt(DV, esl[:], esf[:], esh[:], ALU.subtract), reads=["esf", "esh"], writes=["esl"])
        es_v = es_t.rearrange("p (h q) -> p h q", h=8)
        P.op("dve", cp(DV, es_v[0:1], esh[0:1].unsqueeze(2).to_broadcast([1, 8, 128])), reads=["esh", "es"], writes=["es"])
        P.op("dve", cp(DV, es_v[32:33], esl[32:33].unsqueeze(2).to_broadcast([1, 8, 128])), reads=["esl", "es"], writes=["es"])

        xnb = [None]
        xs4 = [xs[0], xs[1], stg[0], stg[1]]

        def norm_pre(src_dram, slot, xb):
            col = sscol[0] % 96
            sscol[0] += 1
            P.dma(dma(xs4[slot][:], src_dram), writes=[("xs", slot)], semkey="xs%d" % slot)
            P.op("act", act(xnb[0][xb][:], xs4[slot][:], AF.Square, accum=ssb[:, col:col + 1]),
                 reads=[("xs", slot)], writes=[("xn", xb), ("ss", col)])
            P.op("act", act(rsb[:, col:col + 1], ssb[:, col:col + 1], AF.Ln, scale=1.0 / D, bias=EPS),
                 reads=[("ss", col)], writes=[("rs", col)])
            P.op("act", act(rstd[:, col:col + 1], rsb[:, col:col + 1], AF.Exp, scale=-0.5),
                 reads=[("rs", col)], writes=[("rstd", col)])
            P.op("dve", ts(DV, xnb[0][xb][:], xs4[slot][:], rstd[:, col:col + 1], None, ALU.mult),
                 reads=[("xs", slot), ("rstd", col)], writes=[("xn", xb)])

        def norm_tr(xb, hT_dst, pbank, xkey, gT):
            pb = bankb(pbank)
            for k in range(8):
                P.op("pe", tr(pb[:, k * 128:(k + 1) * 128], xnb[0][xb][:, k * 128:(k + 1) * 128], ident_b),
                     reads=[("xn", xb), "cstb"], writes=[("pb", pbank)])
            P.op("dve", tt(DV, hT_dst, pb.rearrange("p (k c) -> p k c", k=8),
                           gT.unsqueeze(2).to_broadcast([128, 8, 128]), ALU.mult),
                 reads=[("pb", pbank), "cst"], writes=[xkey])

        wA = V(R4, [8, 1280], BF16)
        RA = R4 + 20480
        qtm = [V(RA + i * 1024, [512], BF16) for i in range(2)]
        ktm = [V(RA + 2048 + i * 256, [128], BF16) for i in range(2)]
        rt = [[[V(RA + 2560 + ((pp * 2 + w) * 4 + j) * 256, [64], F32) for j in range(4)] for w in range(2)] for pp in range(2)]
        xnb[0] = [xn, V(RA + 6656, [1024], BF16)]
        hTb = [V(R1, [8, 1024], BF16), V(R1 + 16384, [8, 1024], BF16)]
        QT = V(R2, [NT, 512], BF16)
        KT = V(R2 + 32768, [L], BF16)
        Vd = V(R2 + 40960, [NT, 2, 2, 64], BF16)
        Ucm = V(R2 + 57344, [32, 8, 16], BF16)
        X = V(R3, [32, 512], BF16)
        load_w(win_d[:, 0:1280], wA, 1280, 8, "wA", per_chunk_sem=True)
        ropec = C("ropec").rearrange("p (i j) -> p i j", j=8)
        ropes = C("ropes").rearrange("p (i j) -> p i j", j=8)

        def rope(pv, dv_, bshape, i, pkey, rts, rkp, dkey):
            nh = int(np.prod(bshape[1:-1]))
            cb = ropec[:, i, :]
            sb_ = ropes[:, i, :]
            for _ in range(len(bshape) - 2):
                cb = cb.unsqueeze(1)
                sb_ = sb_.unsqueeze(1)
            cb = cb.to_broadcast(bshape)
            sb_ = sb_.to_broadcast(bshape)
            r1, r2 = pv[..., 0:8], pv[..., 8:16]
            if len(bshape) == 4:
                t = [rts[j][:, 0:nh * 8].rearrange("p (a h d) -> p a h d", a=bshape[1], h=bshape[2]) for j in range(4)]
            else:
                t = [rts[j][:, 0:nh * 8].rearrange("p (h d) -> p h d", h=nh) for j in range(4)]
            rk = [rkp + (j,) for j in range(4)]
            P.op("dve", tt(DV, t[0], r1, cb, ALU.mult), reads=[pkey, "cst"], writes=[rk[0]])
            P.op("dve", tt(DV, t[1], r2, sb_, ALU.mult), reads=[pkey], writes=[rk[1]])
            P.op("dve", tt(DV, t[2], r2, cb, ALU.mult), reads=[pkey], writes=[rk[2]])
            P.op("dve", tt(DV, t[3], r1, sb_, ALU.mult), reads=[pkey], writes=[rk[3]])
            P.op("dve", tt(DV, dv_[..., 0:8], t[0], t[1], ALU.subtract), reads=[rk[0], rk[1]], writes=[dkey])
            P.op("dve", tt(DV, dv_[..., 8:16], t[2], t[3], ALU.add), reads=[rk[2], rk[3]], writes=[dkey])
            if len(bshape) == 4:
                for a in range(bshape[1]):
                    P.op("act", cp(A, dv_[:, a, :, 16:64], pv[:, a, :, 16:64]), reads=[pkey], writes=[dkey])
            else:
                P.op("act", cp(A, dv_[..., 16:64], pv[..., 16:64]), reads=[pkey], writes=[dkey])

        def a_pre(i):
            norm_pre(x_d[i * 128:(i + 1) * 128, :], i % 4, i % 2)

        def a_tr(i):
            blk, s8 = i // 8, i % 8
            norm_tr(i % 2, hTb[blk % 2][:, :, s8 * 128:(s8 + 1) * 128], 0, ("hT", blk % 2, s8), C("g1T"))

        def a_mm(i):
            blk, s8 = i // 8, i % 8
            hb = hTb[blk % 2]
            bkv = 1 if i % 2 else 4
            pq, pkv = bank(2 + i % 2), bank(bkv)
            for k in range(8):
                lhs = hb[:, k, s8 * 128:(s8 + 1) * 128]
                P.op("pe", mm(pq, lhs, wA[:, k, 0:512], k == 0, k == 7),
                     reads=[("hT", blk % 2, s8), ("wA", k, 0), ("wA", k, 1024)], writes=[("pb", 2 + i % 2)])
                P.op("pe", mm(pkv[:, 0:256], lhs, wA[:, k, 512:768], k == 0, k == 7),
                     reads=[("hT", blk % 2, s8), ("wA", k, 0), ("wA", k, 1024)], writes=[("pb", bkv)])

        def a_rope(i):
            pp = i % 2
            bkv = 1 if i % 2 else 4
            pq, pkv = bank(2 + i % 2), bank(bkv)
            rope(pq.rearrange("p (a j d) -> p a j d", a=2, j=4),
                 qtm[pp].rearrange("p (j a d) -> p a j d", j=4, a=2), [128, 2, 4, 8], i,
                 ("pb", 2 + i % 2), rt[pp][0], ("rt", pp, 0), ("qtm", pp))
            rope(pkv[:, 0:128].rearrange("p (h d) -> p h d", h=2), ktm[pp].rearrange("p (h d) -> p h d", h=2),
                 [128, 2, 8], i, ("pb", bkv), rt[pp][1], ("rt", pp, 1), ("ktm", pp))
            P.op("act", cp(A, Vd[:, i], pkv[:, 128:256].rearrange("p (h d) -> p h d", h=2).unsqueeze(2)
                           .to_broadcast([128, 2, 2, 64])),
                 reads=[("pb", bkv)], writes=[("Vd", i)])

        def a_qtr(i):
            pp = i % 2
            pQ = bankb(5)
            for j in range(4):
                P.op("pe", tr(pQ[:, j * 128:(j + 1) * 128], qtm[pp][:, j * 128:(j + 1) * 128], ident_b),
                     reads=[("qtm", pp), "cstb"], writes=[("pb", 5)])
            P.op("pe", tr(pQ[:, 512:640], ktm[pp][:], ident_b), reads=[("ktm", pp)], writes=[("pb", 5)])
            P.op("dve", cp(DV, QT[:, i, :], pQ[:, 0:512]), reads=[("pb", 5)], writes=[("QT", i)])
            P.op("act", cp(A, KT[:, i * 128:(i + 1) * 128], pQ[:, 512:640]), reads=[("pb", 5)], writes=[("KT", i)])

        def a_u(blk, s):
            hb = hTb[blk % 2]
            pu = bank(6 + s % 2)
            for k in range(8):
                P.op("pe", mm(pu, hb[:, k, s:1024:8], wA[:, k, 768:1280], k == 0, k == 7),
                     reads=[("hT", blk % 2, j) for j in range(8)] + [("wA", k, 0), ("wA", k, 1024)], writes=[("pb", 6 + s % 2)])
            puv = pu.rearrange("p (g h) -> p g h", g=32)
            if s % 2 == 0:
                P.op("act", cp(A, Ucm[:, :, s, :], puv), reads=[("pb", 6 + s % 2)], writes=[("Ucm", s)])
            else:
                P.op("dve", cp(DV, Ucm[:, :, s, :], puv), reads=[("pb", 6 + s % 2)], writes=[("Ucm", s)])

        def a_xt(blk):
            for g4 in range(4):
                pX = bankb(5)
                for gi in range(8):
                    g = g4 * 8 + gi
                    P.op("pe", tr(pX[:, gi * 128:(gi + 1) * 128], Ucm[:, g].rearrange("p s h -> p (s h)"), ident_b),
                         reads=[("Ucm", s) for s in range(8)], writes=[("pb", 5)])
                P.op("act" if g4 % 2 else "dve",
                     cp(A if g4 % 2 else DV, X[:, g4 * 8:(g4 + 1) * 8, blk * 128:(blk + 1) * 128],
                        pX.rearrange("p (g c) -> p g c", g=8)),
                     reads=[("pb", 5)], writes=[("X", g4)])

        a_pre(0)
        a_tr(0)
        a_pre(1)
        for i in range(NT):
            a_mm(i)
            if i >= 8:
                a_u(i // 8 - 1, i % 8)
            if i + 1 < NT:
                a_tr(i + 1)
            if i + 2 < NT:
                a_pre(i + 2)
            a_rope(i)
            if i >= 1:
                a_qtr(i - 1)
            if i >= 8 and i % 8 == 7:
                a_xt(i // 8 - 1)
        a_qtr(NT - 1)
        for s_ in range(8):
            a_u(3, s_)
        a_xt(3)
        P.barrier()
        PHASE[0] += 1
        if PHASE[0] >= KSTOP:
            P.disabled = True

        attnT = V(R1, [4, L], BF16)
        Pt = [[V(R4 + (b * 3 + d) * 1024, [512], BF16) for d in range(3)] for b in range(2)]
        rD = [V(R4 + 6144 + i * 2048, [512], F32) for i in range(2)]
        lnD = [V(R4 + 10240 + i * 2048, [512], F32) for i in range(2)]
        es_k = es_t.rearrange("p (k q) -> p k q", k=2)
        mb = maskb.rearrange("p (d q) -> p d q", d=2)
        its = [(n, kvh) for n in range(NT) for kvh in range(2)]
        sc_n = [0]

        def att_scores(it):
            n, kvh = its[it]
            pr = slice(kvh * 64, (kvh + 1) * 64)
            for d in (-1, 0, 1):
                if not (0 <= n + d < NT):
                    continue
                bk = sc_n[0] % 4
                sc_n[0] += 1
                pS = bank(bk)
                P.op("pe", mm(pS, KT[pr, (n + d) * 128:(n + d + 1) * 128], QT[pr, n, :], True, True),
                     reads=[], writes=[("pb", bk)])
                P.op("act", act(Pt[it % 2][d + 1][:], pS, AF.Exp, scale=0.125),
                     reads=[("pb", bk)], writes=[("Pt", it % 2, d + 1)])
                if d != 0:
                    P.op("pool", tt(PL, Pt[it % 2][d + 1][:], Pt[it % 2][d + 1][:], mb[:, 0 if d < 0 else 1, :], ALU.mult),
                         reads=[("Pt", it % 2, d + 1)], writes=[("Pt", it % 2, d + 1)])

        def att_out(it):
            n, kvh = its[it]
            dl = [d for d in (-1, 0, 1) if 0 <= n + d < NT]
            bo, bd = 4 + it % 2, 6 + it % 2
            pO, pD = bank(bo), bank(bd)
            for ii, d in enumerate(dl):
                P.op("pe", mm(pD, ones_b[:], Pt[it % 2][d + 1][:], ii == 0, False),
                     reads=[("Pt", it % 2, d + 1)], writes=[("pb", bd)])
            P.op("pe", mm(pD, ones_b[0:33, :], es_k[0:33, kvh, :], False, True), reads=[], writes=[("pb", bd)])
            for ii, d in enumerate(dl):
                P.op("pe", mm(pO, Vd[:, n + d, kvh].rearrange("p a b -> p (a b)"), Pt[it % 2][d + 1][:], ii == 0, ii == len(dl) - 1),
                     reads=[("Pt", it % 2, d + 1)], writes=[("pb", bo)])
            if it % 3 != 2:
                P.op("act", act(lnD[it % 2][:], pD, AF.Ln), reads=[("pb", bd)], writes=[("lnD", it % 2)])
                P.op("act", act(rD[it % 2][:], lnD[it % 2][:], AF.Exp, scale=-1.0), reads=[("lnD", it % 2)], writes=[("rD", it % 2)])
            else:
                P.op("dve", lambda pD=pD, it=it: DV.reciprocal(out=rD[it % 2][:], in_=pD), reads=[("pb", bd)], writes=[("rD", it % 2)])
            pOv = pO.rearrange("p (j q) -> p j q", j=4)
            rDv = rD[it % 2].rearrange("p (j q) -> p j q", j=4)
            for half in range(2):
                rows = slice(half * 64, (half + 1) * 64)
                P.op("dve", tt(DV, attnT[rows, kvh * 2:kvh * 2 + 2, n * 128:(n + 1) * 128],
                               pOv[rows, half:4:2, :], rDv[rows, half:4:2, :], ALU.mult),
                     reads=[("pb", bo), ("rD", it % 2)], writes=[("attnT", n, kvh, half)])

        att_scores(0)
        for it in range(len(its)):
            if it + 1 < len(its):
                att_scores(it + 1)
            att_out(it)
        if debug:
            P.dma(dma(dbg_attn, attnT.rearrange("p a b -> p (a b)")), reads=[("attnT", n, k_, h_) for n in range(NT) for k_ in range(2) for h_ in range(2)], semkey="dbg0")
        P.barrier()
        PHASE[0] += 1
        if PHASE[0] >= KSTOP:
            P.disabled = True

        B1t = V(R4, [64, 16], F32)
        B2t = V(R4 + 4096, [64, 16], F32)
        CT1t = V(R4 + 8192, [32, 16], F32)
        CT2t = V(R4 + 10240, [32, 16], F32)
        TT = [V(R4 + 12288 + i * 4096, [64, 16], F32) for i in range(6)]
        T3i = V(R4 + 12288 + 2 * 4096, [64, 16], I32)
        BB1 = V(XS_OFF, [64, 16], F32)
        BB2 = V(XS_OFF + 4096, [64, 16], F32)
        pa = V(STG_OFF, [8, 8, 16], F32)
        pbt = V(STG_OFF + 4096, [8, 8, 16], F32)
        WsT = [V(R2 + d * 8192, [32, 128], BF16) for d in range(2)]
        Wy = [V(R2 + 16384 + d * 8192, [32, 128], BF16) for d in range(2)]
        Wyp = [V(R2 + 32768 + d * 8192, [32, 128], BF16) for d in range(2)]
        Rr = [V(R2 + 49152 + d * 8192, [32, 128], BF16) for d in range(2)]
        for (dst, src, nm) in ((B1t, B1_d, "B1"), (B2t, B2_d, "B2")):
            P.dma(dma(dst.rearrange("p a b -> p (a b)"), src), writes=[nm], semkey="p" + nm)
        for (dst, src, nm) in ((CT1t, CT1_d, "CT1"), (CT2t, CT2_d, "CT2")):
            P.dma(dma(dst.rearrange("p a b -> p (a b)"), src), writes=[nm], semkey="p" + nm)
        dtv, arv, atv, rdec, f8, tmp64 = [sm64[:, i * 64:(i + 1) * 64] for i in range(6)]
        lamr, lami = C("lamr"), C("lami")
        P.op("act", act(dtv, C("ldt"), AF.Exp), reads=["cst"], writes=["dt"])
        P.op("dve", tt(DV, arv, lamr, dtv, ALU.mult), reads=["dt"], writes=["ar"])
        P.op("dve", stt(atv, lami, float(1 / (2 * math.pi)), dtv, ALU.mult, ALU.mult), reads=["dt"], writes=["at"])
        kv = C("kvals")
        b3 = lambda a: a.unsqueeze(2).to_broadcast([128, 64, 16])
        kb = kv.unsqueeze(1).to_broadcast([128, 64, 16])
        P.op("dve", tt(DV, TT[0], b3(arv), kb, ALU.mult), reads=["ar"], writes=["T0"])
        P.op("act", act(TT[0], TT[0], AF.Exp), reads=["T0"], writes=["T0"])
        P.op("dve", tt(DV, TT[1], b3(atv), kb, ALU.mult), reads=["at"], writes=["T1"])
        P.op("dve", ts(DV, T3i, TT[1], 1.0, None, ALU.mult), reads=["T1"], writes=["T2"])
        P.op("dve", cp(DV, TT[3], T3i), reads=["T2"], writes=["T3"])
        P.op("dve", tt(DV, TT[1], TT[1], TT[3], ALU.subtract), reads=["T1", "T3"], writes=["T1"])
        P.op("act", act(TT[3], TT[1], AF.Abs), reads=["T1"], writes=["T3"])
        P.op("act", act(TT[2], TT[1], AF.Sin, scale=TWO_PI_S), reads=["T1", "T2"], writes=["T2"])
        P.op("act", act(TT[3], TT[3], AF.Sin, scale=-TWO_PI_S, bias=HALF_PI), reads=["T3"], writes=["T3"])
        P.op("dve", tt(DV, TT[3], TT[3], TT[0], ALU.mult), reads=["T3", "T0"], writes=["T3"])
        P.op("dve", tt(DV, TT[2], TT[2], TT[0], ALU.mult), reads=["T2", "T0"], writes=["T2"])
        Er, Ei = TT[3], TT[2]
        P.op("dve", cp(DV, rdec, TT[0][:, :, 15]), reads=["T0"], writes=["rdec"])
        P.op("dve", cp(DV, f8, TT[1][:, :, 15]), reads=["T1"], writes=["f8"])

        s6 = V(R4 + 36864, [64 * 8], F32)
        xr, den, t1, t2, ber, bei, beS1, beS2 = [s6[:, i * 64:(i + 1) * 64] for i in range(8)]
        E1r, E1i = Er[:, :, 8], Ei[:, :, 8]
        P.op("dve", ts(DV, xr, E1r, -1.0, None, ALU.add), reads=["T3"], writes=["xr"])
        P.op("dve", tt(DV, den, lamr, lamr, ALU.mult), reads=["cst"], writes=["den"])
        P.op("dve", tt(DV, t1, lami, lami, ALU.mult), reads=["cst"], writes=["t1"])
        P.op("dve", tt(DV, den, den, t1, ALU.add), reads=["den", "t1"], writes=["den"])
        P.op("dve", lambda: DV.reciprocal(out=den, in_=den), reads=["den"], writes=["den"])
        P.op("dve", tt(DV, t1, xr, lamr, ALU.mult), reads=["xr", "t1"], writes=["t1"])
        P.op("dve", tt(DV, t2, E1i, lami, ALU.mult), reads=["T2"], writes=["t2"])
        P.op("dve", tt(DV, t1, t1, t2, ALU.add), reads=["t1", "t2"], writes=["t1"])
        P.op("dve", tt(DV, ber, t1, den, ALU.mult), reads=["t1", "den"], writes=["ber"])
        P.op("dve", tt(DV, t1, E1i, lamr, ALU.mult), reads=["T2", "ber"], writes=["t1"])
        P.op("dve", tt(DV, t2, xr, lami, ALU.mult), reads=["xr", "t2"], writes=["t2"])
        P.op("dve", tt(DV, t1, t1, t2, ALU.subtract), reads=["t1", "t2"], writes=["t1"])
        P.op("dve", tt(DV, bei, t1, den, ALU.mult), reads=["t1", "den"], writes=["bei"])
        sg1, sg2 = C("sgn1"), C("sgn2")
        P.op("dve", ts(DV, beS1, bei, sg1, None, ALU.mult), reads=["bei"], writes=["beS1"])
        P.op("dve", ts(DV, beS2, bei, sg2, None, ALU.mult), reads=["bei"], writes=["beS2"])
        bh = lambda a: a.unsqueeze(2).to_broadcast([128, 64, 16])
        P.op("dve", tt(DV, BB1, B1t, bh(ber), ALU.mult), reads=["B1", "ber"], writes=["BB1"])
        P.op("dve", tt(DV, TT[4], B2t, bh(beS1), ALU.mult), reads=["B2", "beS1"], writes=["T4"])
        P.op("dve", tt(DV, BB1, BB1, TT[4], ALU.add), reads=["BB1", "T4"], writes=["BB1"])
        P.op("dve", tt(DV, BB2, B2t, bh(ber), ALU.mult), reads=["B2", "ber"], writes=["BB2"])
        P.op("dve", tt(DV, TT[4], B1t, bh(beS2), ALU.mult), reads=["B1", "beS2", "BB1"], writes=["T4"])
        P.op("dve", tt(DV, BB2, BB2, TT[4], ALU.add), reads=["BB2", "T4"], writes=["BB2"])
        P.op("dve", ts(DV, TT[0], Er, sg2, None, ALU.mult), reads=["T3", "rdec"], writes=["T0"])
        P.op("dve", ts(DV, TT[1], Ei, sg1, None, ALU.mult), reads=["T2", "f8"], writes=["T1"])
        P.op("dve", ts(DV, TT[4], Ei, -1.0, None, ALU.mult), reads=["T2", "BB2"], writes=["T4"])
        P.op("dve", ts(DV, TT[5], Er, -1.0, None, ALU.mult), reads=["T3"], writes=["T5"])
        ErA, EiS, EiN, ErN = TT[0], TT[1], TT[4], TT[5]

        def ksel(tab, dg0, k0, rev):
            v = tab[:, dg0:dg0 + 8, :]
            v = v[:, :, k0:k0 - 8 if k0 - 8 >= 0 else None:-1] if rev else v[:, :, k0:k0 + 8]
            return v.unsqueeze(3).to_broadcast([128, 8, 8, 16])

        def build_tab(dst, d, g0, tA, tB, cA, cB, k0, rev, is_b):
            dg0 = d * 32 + g0
            if is_b:
                ca = cA[:, dg0:dg0 + 8, :].unsqueeze(2).to_broadcast([128, 8, 8, 16])
                cb_ = cB[:, dg0:dg0 + 8, :].unsqueeze(2).to_broadcast([128, 8, 8, 16])
            else:
                ca = cA[:, g0:g0 + 8, :].unsqueeze(2).to_broadcast([128, 8, 8, 16])
                cb_ = cB[:, g0:g0 + 8, :].unsqueeze(2).to_broadcast([128, 8, 8, 16])
            P.op("dve", tt(DV, pa, ksel(tA, dg0, k0, rev), ca, ALU.mult), reads=["T0", "T1", "T4", "T5", "T2", "T3", "BB1", "BB2", "CT1", "CT2"], writes=["pa"])
            P.op("dve", tt(DV, pbt, ksel(tB, dg0, k0, rev), cb_, ALU.mult), reads=["T0", "T1", "T4", "T5", "T2", "T3", "BB1", "BB2", "CT1", "CT2"], writes=["pbt"])
            P.op("dve", tt(DV, dst[:, g0:g0 + 8, :].rearrange("p g (s h) -> p g s h", s=8), pa, pbt, ALU.add),
                 reads=["pa", "pbt"], writes=[("tabs", d, g0, id(dst))])

        MgA = V(R4, [32, 128], BF16)
        TL = R4 + 38912
        Mt = V(TL, [128], F32)
        Mt2 = V(TL + 512, [128], F32)
        pMs = [V(TL + 1024 + i * 512, [128], F32) for i in range(2)]
        Jt = V(TL + 2048, [128], BF16)
        P.dma(dma(Jt[:], jp_d), writes=["Jt"], semkey="jp")
        smf, smb = C("smf"), C("smb")
        dcol = C("dcol")
        P.op("pool", lambda: PL.nop(), reads=["BB1", "BB2"], writes=["MgA"])

        def prepass(g):
            bb = 2 if g % 2 == 0 else 4
            for d in range(2):
                pW = bankb(bb + d)[:, 512:640]
                pM = bank(bb + d)
                tk_ = [("tabs", d, (g // 8) * 8, id(WsT[d])), ("tabs", d, (g // 8) * 8, id(Rr[d]))]
                P.op("pe", tr(pW, WsT[d][:, g, :], ident_b), reads=tk_, writes=[("pb", bb + d)])
                P.op("pe", mm(pM[:, 0:128], WsT[d][:, g, :], Rr[d][:, g, :], True, True), reads=tk_, writes=[("pb", bb + d)])
                P.op("act", cp(A, WsT[d][:, g, :], pW), reads=[("pb", bb + d)], writes=[("Ws", d, g)])
                P.op("act", cp(A, Rr[d][:, g, 0:64], pW[:, 64:128]), reads=[("pb", bb + d)], writes=[("Wsp", d, g)])
                P.op("act", act(Rr[d][:, g, 64:128], pW[:, 0:64], AF.Copy, scale=-1.0), reads=[("pb", bb + d)], writes=[("Wsp", d, g)])
                P.op("act", cp(A, pMs[d][:], pM[:, 0:128]), reads=[("pb", bb + d)], writes=[("pMs", d)])
            P.op("dve", tt(DV, Mt[:], pMs[0][:], smf, ALU.mult), reads=[("pMs", 0)], writes=["Mt"])
            P.op("dve", tt(DV, Mt2[:], pMs[1][:], smb, ALU.mult), reads=[("pMs", 1)], writes=["Mt2"])
            P.op("dve", tt(DV, Mt[:], Mt[:], Mt2[:], ALU.add), reads=["Mt", "Mt2"], writes=["Mt"])
            P.op("dve", stt(MgA[:, g, :], ident_f, dcol[:, g:g + 1], Mt[:], ALU.mult, ALU.add), reads=["Mt", "MgA"], writes=[("Mg", g)])

        for g0 in range(0, 32, 8):
            for d in range(2):
                build_tab(WsT[d], d, g0, Er, EiS, BB1, BB2, 14 if d == 0 else 7, d == 0, True)
                build_tab(Wy[d], d, g0, ErA, EiN, CT1t, CT2t, 8 if d == 0 else 15, d == 1, False)
                for hh_ in range(2):
                    bj = 6 + hh_
                    gsl = slice(g0 + 4 * hh_, g0 + 4 * hh_ + 4)
                    P.op("pe", mm(bank(bj), Jt[:], Wy[d][:, gsl, :].rearrange("p g c -> p (g c)"), True, True),
                         reads=[("tabs", d, g0, id(Wy[d])), "Jt"], writes=[("pb", bj)])
                    P.op("act", cp(A, Wyp[d][:, gsl, :].rearrange("p g c -> p (g c)"), bank(bj)),
                         reads=[("pb", bj)], writes=[("tabs", d, g0, id(Wyp[d]), hh_)])
                build_tab(Rr[d], d, g0, ErA, EiN, CT1t, CT2t, 0 if d == 0 else 7, d == 1, False)
            if g0 >= 8:
                for g in range(g0 - 8, g0):
                    prepass(g)
        for g in range(24, 32):
            prepass(g)
        P.barrier()
        PHASE[0] += 1
        if PHASE[0] >= KSTOP:
            P.disabled = True

        Wsg = lambda d, g: WsT[d][:, g, :]
        Wspg = lambda d, g: Rr[d][:, g, :]
        GcS = [V(R4 + 8192 + d * 1024, [512], BF16) for d in range(2)]
        GsS = [V(R4 + 10240 + d * 1024, [512], BF16) for d in range(2)]
        WB = R4 + 12288
        sTb = [[V(WB + (pp * 2 + d) * 8192, [512], F32) for d in range(2)] for pp in range(2)]
        cTb = [[V(WB + 2048 + (pp * 2 + d) * 8192, [512], F32) for d in range(2)] for pp in range(2)]
        Aab = [[V(WB + 4096 + (pp * 2 + d) * 8192, [512], F32) for d in range(2)] for pp in range(2)]
        Bbb = [[V(WB + 6144 + (pp * 2 + d) * 8192, [512], F32) for d in range(2)] for pp in range(2)]
        assert WB + 32768 <= ARENA_B
        ycm = X
        iota = C("iota")
        for d in range(2):
            P.op("pool", lambda d=d: PL.memset(GcS[d][:], 0.0), writes=[("Gc", d)])
            P.op("pool", lambda d=d: PL.memset(GsS[d][:], 0.0), writes=[("Gs", d)])

        def stage_Ta(g, d):
            dg = d * 32 + g
            pp = g % 2
            sT, cT = sTb[pp][d], cTb[pp][d]
            P.op("act", act(cT[:].bitcast(I32), iota, AF.Identity, scale=f8[:, dg:dg + 1]), reads=[], writes=[("cT", pp, d)])
            P.op("act", act(sT[:], cT[:].bitcast(I32), AF.Identity), reads=[("cT", pp, d)], writes=[("sT", pp, d)])

        def stage_Tb(g, d):
            dg = d * 32 + g
            pp = g % 2
            sT, cT = sTb[pp][d], cTb[pp][d]
            P.op("dve", stt(sT[:], iota, f8[:, dg:dg + 1], sT[:], ALU.mult, ALU.subtract), reads=[("sT", pp, d)], writes=[("sT", pp, d)])
            P.op("act", act(cT[:], sT[:], AF.Abs), reads=[("sT", pp, d)], writes=[("cT", pp, d)])
            P.op("act", act(sT[:], sT[:], AF.Sin, scale=TWO_PI_S), reads=[("sT", pp, d)], writes=[("sT", pp, d)])
            P.op("act", act(cT[:], cT[:], AF.Sin, scale=-TWO_PI_S, bias=HALF_PI), reads=[("cT", pp, d)], writes=[("cT", pp, d)])

        def stage_S(g, d):
            pS, pSp = bank(4 + 2 * d), bank(5 + 2 * d)
            P.op("pe", mm(pS, Wsg(d, g), X[:, g, :], True, True), reads=[("X", g)], writes=[("pb", 4 + 2 * d)])
            P.op("pe", mm(pSp, Wspg(d, g), X[:, g, :], True, True), reads=[("X", g)], writes=[("pb", 5 + 2 * d)])

        def stage_M1(g, d):
            pp = g % 2
            pS, pSp = bank(4 + 2 * d), bank(5 + 2 * d)
            sT, cT, Aa, Bb = sTb[pp][d], cTb[pp][d], Aab[pp][d], Bbb[pp][d]
            rv = (lambda a: a[:, ::-1]) if d == 1 else (lambda a: a)
            P.op("dve", tt(DV, Aa[:], pS, rv(cT[:]), ALU.mult), reads=[("pb", 4 + 2 * d), ("cT", pp, d)], writes=[("Aa", pp, d)])
            P.op("dve", tt(DV, Bb[:], pSp, rv(sT[:]), ALU.mult), reads=[("pb", 5 + 2 * d), ("sT", pp, d)], writes=[("Bb", pp, d)])
            if d == 0:
                P.op("dve", tt(DV, Aa[:], Aa[:], Bb[:], ALU.add), reads=[("Aa", pp, d), ("Bb", pp, d)], writes=[("Aa", pp, d)])
            else:
                P.op("pool", tt(PL, Aa[:], Aa[:], Bb[:], ALU.add), reads=[("Aa", pp, d), ("Bb", pp, d)], writes=[("Aa", pp, d)])

        def stage_M2(g, d):
            dg = d * 32 + g
            pp = g % 2
            sT, cT, Aa, Bb = sTb[pp][d], cTb[pp][d], Aab[pp][d], Bbb[pp][d]
            Gt = Bb
            rv = (lambda a: a[:, ::-1]) if d == 1 else (lambda a: a)
            P.op("dve", lambda: DV.tensor_tensor_scan(
                out=rv(Gt[:]), data0=rdec[:, dg:dg + 1].to_broadcast([128, 512]), data1=rv(Aa[:]),
                initial=0.0, op0=ALU.mult, op1=ALU.add), reads=[("Aa", pp, d), ("Bb", pp, d)], writes=[("Bb", pp, d)])
            if d == 0:
                P.op("pool", tt(PL, GcS[d][:, 1:512], Gt[:, 0:511], cT[:, 0:511], ALU.mult),
                     reads=[("Bb", pp, d), ("cT", pp, d)], writes=[("Gc", d)])
                P.op("pool", tt(PL, GsS[d][:, 1:512], Gt[:, 0:511], sT[:, 0:511], ALU.mult),
                     reads=[("Bb", pp, d), ("sT", pp, d)], writes=[("Gs", d)])
            else:
                P.op("pool", tt(PL, GcS[d][:, 0:511], Gt[:, 1:512], cT[:, 510::-1], ALU.mult),
                     reads=[("Bb", pp, d), ("cT", pp, d)], writes=[("Gc", d)])
                P.op("pool", tt(PL, GsS[d][:, 0:511], Gt[:, 1:512], sT[:, 510::-1], ALU.mult),
                     reads=[("Bb", pp, d), ("sT", pp, d)], writes=[("Gs", d)])

        def stage_O(g):
            pY = bank(g % 2)
            for cb in range(4):
                cs = slice(cb * 128, (cb + 1) * 128)
                o = pY[:, cb * 128:(cb + 1) * 128]
                P.op("pe", mm(o, X[:, g, cs], MgA[:, g, :], True, False), reads=[("X", g)], writes=[("pb", g % 2)])
                for d in range(2):
                    P.op("pe", mm(o, GcS[d][:, cs], Wy[d][:, g, :], False, False), reads=[("Gc", d)], writes=[("pb", g % 2)])
                    P.op("pe", mm(o, GsS[d][:, cs], Wyp[d][:, g, :], False, d == 1), reads=[("Gs", d)], writes=[("pb", g % 2)])

        def stage_E(g):
            P.op("act", cp(A, ycm[:, g, :], bank(g % 2)), reads=[("pb", g % 2)], writes=[("X", g)])

        for d in range(2):
            stage_Ta(0, d)
            stage_Tb(0, d)
        for d in range(2):
            stage_S(0, d)
        for g in range(32):
            if g + 1 < 32:
                for d in range(2):
                    stage_Ta(g + 1, d)
            stage_M1(g, 0)
            stage_M1(g, 1)
            if g + 1 < 32:
                for d in range(2):
                    stage_Tb(g + 1, d)
            stage_M2(g, 0)
            stage_M2(g, 1)
            if g + 1 < 32:
                for d in range(2):
                    stage_S(g + 1, d)
            stage_O(g)
            if g >= 1:
                stage_E(g - 1)
        stage_E(31)
        P.barrier()
        PHASE[0] += 1
        if PHASE[0] >= KSTOP:
            P.disabled = True

        yT = V(R2, [4, L], BF16)
        ycm2 = [V(XS_OFF + b * 8192, [8, 512], BF16) for b in range(2)] + \
               [V(R4 + 8192 + b * 8192, [8, 512], BF16) for b in range(2)]
        yv5 = ycm.rearrange("p g (b t h) -> p g b t h", b=4, t=8)
        for blk in range(4):
            for t in range(8):
                P.op("act", act(ycm2[blk][:, t, :].rearrange("p (g h) -> p g h", g=32), yv5[:, :, blk, t, :], AF.Gelu_apprx_tanh),
                     reads=[], writes=[("ycm", blk, t)])
        P.op("pool", lambda: PL.nop(), reads=[("ycm", blk, t) for blk in range(4) for t in range(8)])
        wg = V(R3, [8, 2048], BF16)
        wab = V(R2 + 32768, [4, 1024], BF16)
        wsb = V(R2 + 40960, [4, 1024], BF16)
        wout = V(R2 + 49152, [8, 1024], BF16)
        load_w(win_d[:, 1280:3328], wg, 2048, 8, "wg")
        load_w(wab_d, wab, 1024, 4, "wab")
        load_w(wsb_d, wsb, 1024, 4, "wsb")
        load_w(wout_d, wout, 1024, 8, "wout")
        wglu = V(R4, [4, 512], BF16)
        sgl = [V(R4 + 4096 + i * 1024, [512], BF16) for i in range(4)]
        load_w(wglu_d, wglu, 512, 4, "wglu")
        for blk in range(4):
            for q4 in range(4):
                pZ = bankb((blk * 4 + q4) % 2)
                for t in range(8):
                    P.op("pe", tr(pZ[:, t * 128:(t + 1) * 128], ycm2[blk][:, t, q4 * 128:(q4 + 1) * 128], ident_b),
                         reads=[("ycm", blk, t)], writes=[("pb", (blk * 4 + q4) % 2)])
                P.op("dve" if q4 % 2 else "act",
                     cp(DV if q4 % 2 else A, yT[:, q4, blk * 1024:(blk + 1) * 1024].rearrange("p (c t) -> p t c", t=8),
                        pZ.rearrange("p (t c) -> p t c", t=8)),
                     reads=[("pb", (blk * 4 + q4) % 2)], writes=[("yT", blk)])
        for n5 in range(8):
            cs = slice(n5 * 512, (n5 + 1) * 512)
            for mo in range(4):
                pg = bank(2 + mo)
                for ki in range(4):
                    P.op("pe", mm(pg, wglu[:, ki, mo * 128:(mo + 1) * 128], yT[:, ki, cs], ki == 0, ki == 3),
                         reads=wk_all("wglu", 4) + [("yT", n5 // 2)], writes=[("pb", 2 + mo)])
                P.op("act", act(sgl[mo][:], pg, AF.Sigmoid), reads=[("pb", 2 + mo)], writes=[("sgl", mo)])
            for mo in range(4):
                P.op("dve", tt(DV, yT[:, mo, cs], yT[:, mo, cs], sgl[mo][:], ALU.mult),
                     reads=[("sgl", mo), ("yT", n5 // 2)], writes=[("yT", n5 // 2)])
        if debug:
            P.dma(dma(dbg_ssm, yT.rearrange("p a b -> p (a b)")), reads=[("yT", b) for b in range(4)], semkey="dbg1")
        P.barrier()
        PHASE[0] += 1
        if PHASE[0] >= KSTOP:
            P.disabled = True

        TB = 256
        hT5 = [V(R4 + i * 4096, [8, TB], BF16) for i in range(2)]
        sgt = [V(R4 + 8192 + i * 512, [TB], BF16) for i in range(2)]
        t12 = [V(R4 + 9216 + i * 1024, [TB], F32) for i in range(2)]
        mgT = V(R4 + 11264, [8, TB], BF16)
        x2t = [V(R4 + 15360 + i * 4096, [1024], F32) for i in range(2)]
        xnb[0] = [xn, V(R4 + 23552, [1024], BF16)]
        NT2 = L // TB

        def b1_pre(n2):
            for sub in range(2):
                i = n2 * 2 + sub
                norm_pre(x_d[i * 128:(i + 1) * 128, :], (n2 % 2) * 2 + sub, sub)

        def b1_tr(n2):
            for sub in range(2):
                norm_tr(sub, hT5[n2 % 2][:, :, sub * 128:(sub + 1) * 128], 6 + sub, ("hT5", n2 % 2, sub), C("g1T"))

        b1_pre(0)
        b1_tr(0)
        for n2 in range(NT2):
            cs = slice(n2 * TB, (n2 + 1) * TB)
            hh = hT5[n2 % 2]
            hk = [("hT5", n2 % 2, 0), ("hT5", n2 % 2, 1)]
            for mo in range(8):
                for br in range(2):
                    pg = bank(br)
                    for k in range(8):
                        P.op("pe", mm(pg[:, 0:TB], wg[:, k, (br * 8 + mo) * 128:(br * 8 + mo + 1) * 128], hh[:, k, :], k == 0, k == 7),
                             reads=hk + wk_all("wg", 8), writes=[("pb", br)])
                    P.op("act", act(sgt[br][:], pg[:, 0:TB], AF.Sigmoid), reads=[("pb", br)], writes=[("sgt", br)])
                    pb_ = bank(2 + br)
                    wbr, src, wkey = (wab, attnT, "wab") if br == 0 else (wsb, yT, "wsb")
                    for ki in range(4):
                        P.op("pe", mm(pb_[:, 0:TB], wbr[:, ki, mo * 128:(mo + 1) * 128], src[:, ki, cs], ki == 0, ki == 3),
                             reads=wk_all(wkey, 4), writes=[("pb", 2 + br)])
                    P.op("dve", tt(DV, t12[br][:], pb_[:, 0:TB], sgt[br][:], ALU.mult),
                         reads=[("pb", 2 + br), ("sgt", br)], writes=[("t12", br)])
                P.op("dve", tt(DV, mgT[:, mo, :], t12[0][:], t12[1][:], ALU.add),
                     reads=[("t12", 0), ("t12", 1)], writes=[("mgT", mo)])
                if n2 + 1 < NT2 and mo == 1:
                    b1_pre(n2 + 1)
                if n2 + 1 < NT2 and mo == 5:
                    b1_tr(n2 + 1)
            for sub in range(2):
                i = n2 * 2 + sub
                slot = (n2 % 2) * 2 + sub
                for half in range(2):
                    po_ = bank(4 + half)
                    for k in range(8):
                        P.op("pe", mm(po_, mgT[:, k, sub * 128:(sub + 1) * 128], wout[:, k, half * 512:(half + 1) * 512], k == 0, k == 7),
                             reads=[("mgT", k)] + wk_all("wout", 8), writes=[("pb", 4 + half)])
                    P.op("dve", tt(DV, x2t[sub][:, half * 512:(half + 1) * 512], po_, xs4[slot][:, half * 512:(half + 1) * 512], ALU.add),
                         reads=[("pb", 4 + half), ("xs", slot)], writes=[("x2t", sub)])
                P.dma(dma(x2_d[i * 128:(i + 1) * 128, :], x2t[sub][:]), reads=[("x2t", sub)], writes=[("x2d", i)], semkey="x2w%d" % sub)
        P.barrier()
        PHASE[0] += 1
        if PHASE[0] >= KSTOP:
            P.disabled = True

        wfg = V(R1, [8, DFF], BF16)
        wfu = V(R1 + 45056, [8, DFF], BF16)
        wfd = V(R1 + 90112, [NMF, 1024], BF16)
        W2 = R1 + 135168
        assert W2 >= R4 and W2 - R4 <= 4096
        gft = V(W2, [1024], F32)
        h2T = [V(W2 + 4096 + i * 4096, [8, TB], BF16) for i in range(2)]
        aT = V(W2 + 12288, [NMF, TB], BF16)
        sgf = [V(W2 + 23552 + i * 1024, [TB], F32) for i in range(2)]
        x3t = [V(W2 + 25600 + i * 4096, [1024], F32) for i in range(2)]
        junk = V(W2 + 33792, [1024], BF16)
        xnb[0] = [xn, V(W2 + 35840, [1024], BF16)]
        assert W2 + 37888 <= ARENA_B
        P.dma(dma(gft[:], gf_d), writes=["gft"], semkey="gf")
        for c0 in range(0, DFF, 1024):
            load_w(wfg_d, wfg, DFF, 8, "wfg", per_piece_sem=True, pieces=[(kc, c0) for kc in range(8)])
            load_w(wfu_d, wfu, DFF, 8, "wfu", per_piece_sem=True, pieces=[(kc, c0) for kc in range(8)])
        load_w(wfd_d, wfd, 1024, NMF, "wfd", per_piece_sem=True)

        def b2_pre(n2):
            for sub in range(2):
                i = n2 * 2 + sub
                norm_pre(x2_d[i * 128:(i + 1) * 128, :], (n2 % 2) * 2 + sub, sub)

        def b2_tr(n2):
            for sub in range(2):
                norm_tr(sub, h2T[n2 % 2][:, :, sub * 128:(sub + 1) * 128], 6 + sub, ("h2T", n2 % 2, sub), C("g2T"))

        b2_pre(0)
        b2_tr(0)
        for n2 in range(NT2):
            hh = h2T[n2 % 2]
            hk = [("h2T", n2 % 2, 0), ("h2T", n2 % 2, 1)]
            for mo in range(NMF):
                pg, pu_ = bank(mo % 2), bank(2 + mo % 2)
                ms = slice(mo * 128, (mo + 1) * 128)
                for k in range(8):
                    P.op("pe", mm(pg[:, 0:TB], wfg[:, k, ms], hh[:, k, :], k == 0, k == 7),
                         reads=hk + [("wfg", kc_, (mo * 128) // 1024 * 1024) for kc_ in range(8)], writes=[("pb", mo % 2)])
                for k in range(8):
                    P.op("pe", mm(pu_[:, 0:TB], wfu[:, k, ms], hh[:, k, :], k == 0, k == 7),
                         reads=hk + [("wfu", kc_, (mo * 128) // 1024 * 1024) for kc_ in range(8)], writes=[("pb", 2 + mo % 2)])
                P.op("act", act(sgf[mo % 2][:], pg[:, 0:TB], AF.Silu), reads=[("pb", mo % 2)], writes=[("sgf", mo % 2)])
                P.op("dve", tt(DV, aT[:, mo, :], pu_[:, 0:TB], sgf[mo % 2][:], ALU.mult),
                     reads=[("pb", 2 + mo % 2), ("sgf", mo % 2)], writes=[("aT", mo)])
                if n2 + 1 < NT2 and mo == 4:
                    b2_pre(n2 + 1)
                if n2 + 1 < NT2 and mo == 14:
                    b2_tr(n2 + 1)
            for sub in range(2):
                i = n2 * 2 + sub
                slot = (n2 % 2) * 2 + sub
                for half in range(2):
                    po_ = bank(4 + half)
                    for mo in range(NMF):
                        P.op("pe", mm(po_, aT[:, mo, sub * 128:(sub + 1) * 128], wfd[:, mo, half * 512:(half + 1) * 512], mo == 0, mo == NMF - 1),
                             reads=[("aT", mo)] + [("wfd", m_, 0) for m_ in range((mo // 6) * 6, min(NMF, (mo // 6) * 6 + 6))], writes=[("pb", 4 + half)])
                    P.op("dve", tt(DV, x3t[sub][:, half * 512:(half + 1) * 512], po_, xs4[slot][:, half * 512:(half + 1) * 512], ALU.add),
                         reads=[("pb", 4 + half), ("xs", slot)], writes=[("x3t", sub)])
                col = sscol[0] % 96
                sscol[0] += 1
                P.op("act", act(junk[:], x3t[sub][:], AF.Square, accum=ssb[:, col:col + 1]),
                     reads=[("x3t", sub)], writes=["junk", ("ss", col)])
                P.op("act", act(rsb[:, col:col + 1], ssb[:, col:col + 1], AF.Ln, scale=1.0 / D, bias=EPS),
                     reads=[("ss", col)], writes=[("rs", col)])
                P.op("act", act(rstd[:, col:col + 1], rsb[:, col:col + 1], AF.Exp, scale=-0.5),
                     reads=[("rs", col)], writes=[("rstd", col)])
                P.op("dve", stt(x3t[sub][:], x3t[sub][:], rstd[:, col:col + 1], gft[:], ALU.mult, ALU.mult),
                     reads=[("x3t", sub), ("rstd", col), "gft"], writes=[("x3t", sub)])
                P.dma(dma(y_d[i * 128:(i + 1) * 128, :], x3t[sub][:]), reads=[("x3t", sub)], semkey="yw%d" % sub)
        stats = P.emit()
        print("PROG", stats)
    return nc


def _consts():
    cst = np.zeros((128, NCST), np.float32)

    def put(name, arr):
        o, w = CST[name]
        cst[:, o:o + w] = arr.reshape(128, w)
    put("ident", np.eye(128, dtype=np.float32))
    pos = np.arange(L, dtype=np.float32)
    inv_freq = (np.float32(500000.0) ** (-np.arange(0, 16, 2, dtype=np.float32) / np.float32(16))).astype(np.float32)
    ang = pos[:, None] * inv_freq[None, :]
    cosv = np.cos(ang).astype(np.float32).reshape(NT, 128, 8).transpose(1, 0, 2)
    sinv = np.sin(ang).astype(np.float32).reshape(NT, 128, 8).transpose(1, 0, 2)
    put("ropec", np.ascontiguousarray(cosv))
    put("ropes", np.ascontiguousarray(sinv))
    s_i = np.repeat(np.arange(8), 16)
    put("smf", (s_i[:, None] <= s_i[None, :]).astype(np.float32))
    put("smb", (s_i[:, None] >= s_i[None, :]).astype(np.float32))
    put("iota", np.broadcast_to(np.arange(512, dtype=np.float32), (128, 512)).copy())
    put("kvals", np.broadcast_to(np.arange(-7, 9, dtype=np.float32), (128, 16)).copy())
    sg = np.ones((128, 1), np.float32)
    sg[:64] = -1
    put("sgn1", sg)
    put("sgn2", -sg)
    cstb = np.zeros((128, NCSTB), np.float32)
    cstb[:, 0:128] = np.eye(128)
    kk = np.arange(128)[:, None]
    qq = np.arange(128)[None, :]
    m_prev = np.where(qq <= kk, 1.0, 0.0)
    m_next = np.where(kk <= qq, 1.0, 0.0)
    cstb[:, 128:128 + 512] = np.tile(m_prev, (1, 4))
    cstb[:, 640:640 + 512] = np.tile(m_next, (1, 4))
    return cst, cstb.astype(ml_dtypes.bfloat16)


def _jperm():
    J = np.zeros((128, 128), np.float32)
    for p in range(64):
        J[64 + p, p] = 1.0
        J[p, 64 + p] = -1.0
    return J.astype(ml_dtypes.bfloat16)


_NC_CACHE = {}


def kernel(x, norm1_g, w_in, attn_sink, ssm_lambda_re, ssm_lambda_im, ssm_log_dt,
           ssm_b_re, ssm_b_im, ssm_c_re, ssm_c_im, ssm_d, w_glu, w_attn_branch,
           w_ssm_branch, w_out, norm2_g, w_ffn_gate, w_ffn_up, w_ffn_down, norm_f_g):
    f = lambda a: np.ascontiguousarray(np.asarray(a, dtype=np.float32))
    x = f(x)
    cst, cstb = _consts()

    def put(name, arr):
        o, w = CST[name]
        cst[:, o:o + w] = np.asarray(arr, np.float32).reshape(128, w)
    put("g1T", f(norm1_g)[0].reshape(8, 128).T)
    put("g2T", f(norm2_g)[0].reshape(8, 128).T)
    put("sink", np.broadcast_to(f(attn_sink)[0], (128, 8)))
    put("dcol", np.tile(f(ssm_d)[0].T, (8, 1)))
    lr = f(ssm_lambda_re)[0].reshape(64, 64).T
    li = f(ssm_lambda_im)[0].reshape(64, 64).T
    put("lamr", np.concatenate([lr, lr], 0))
    put("lami", np.concatenate([li, li], 0))
    put("ldt", np.broadcast_to(f(ssm_log_dt)[0].reshape(1, 64), (128, 64)))
    brT = f(ssm_b_re)[0].transpose(2, 0, 1, 3).reshape(64, 1024)
    biT = f(ssm_b_im)[0].transpose(2, 0, 1, 3).reshape(64, 1024)
    crT = f(ssm_c_re)[0].transpose(2, 0, 1).reshape(64, 512)
    ciT = f(ssm_c_im)[0].transpose(2, 0, 1).reshape(64, 512)
    shared = {
        "w_in": f(w_in)[0], "w_glu": f(w_glu)[0], "w_ab": f(w_attn_branch)[0], "w_sb": f(w_ssm_branch)[0],
        "w_out": f(w_out)[0], "w_fg": f(w_ffn_gate)[0], "w_fu": f(w_ffn_up)[0], "w_fd": f(w_ffn_down)[0],
        "gfb": np.ascontiguousarray(np.broadcast_to(f(norm_f_g).reshape(1, D), (128, D))),
        "cst": cst, "cstb": cstb,
        "B1": np.ascontiguousarray(np.concatenate([brT, biT], 0)),
        "B2": np.ascontiguousarray(np.concatenate([biT, brT], 0)),
        "CT1": np.ascontiguousarray(np.concatenate([crT, ciT], 0)),
        "CT2": np.ascontiguousarray(np.concatenate([ciT, crT], 0)),
        "jperm": _jperm(),
    }
    if "nc" not in _NC_CACHE:
        _NC_CACHE["nc"] = build()
    nc = _NC_CACHE["nc"]
    in_maps = [dict(shared, x=x[b]) for b in range(8)]
    res = run_bass_kernel_spmd(nc, in_maps, core_ids=list(range(8)))
    return np.stack([np.asarray(r["y"], dtype=np.float32) for r in res.results], axis=0)
```

```python
import math
import os
from contextlib import ExitStack
import numpy as np
import ml_dtypes
import concourse.bass as bass
import concourse.mybir as mybir
from concourse.bass_utils import run_bass_kernel_spmd

F32 = mybir.dt.float32
BF16 = mybir.dt.bfloat16
I32 = mybir.dt.int32
ALU = mybir.AluOpType
AF = mybir.ActivationFunctionType

L = 4096
D = 1024
NT = 32
DFF = 2816
NMF = 22
EPS = 1e-6
TWO_PI_S = float(2 * math.pi * (1 - 2e-6))
HALF_PI = float(math.pi / 2)


class Prog:
    ENG = ("pe", "act", "dve", "pool", "sp")

    def __init__(self, nc, stack):
        self.nc = nc
        self.stack = stack
        self.ops = []
        self.last_w = {}
        self.readers = {}
        self.sems = {}
        self.barrier_idx = None
        self.last_eng = {}
        self.dma_since = []

    def eng(self, name):
        nc = self.nc
        return {"pe": nc.tensor, "act": nc.scalar, "dve": nc.vector,
                "pool": nc.gpsimd, "sp": nc.sync}[name]

    def sem(self, key):
        if key not in self.sems:
            nm = "s_" + "_".join(str(x) for x in key)
            self.sems[key] = self.stack.enter_context(self.nc.semaphore(nm))
        return self.sems[key]

    def op(self, engine, fn, reads=(), writes=(), semkey=None, extra_deps=()):
        if getattr(self, "disabled", False):
            return -1
        idx = len(self.ops)
        if any(isinstance(k, tuple) and k and k[0] == "pb" for k in reads):
            writes = list(writes) + [k for k in reads if isinstance(k, tuple) and k and k[0] == "pb"]
            reads = [k for k in reads if not (isinstance(k, tuple) and k and k[0] == "pb")]
        deps = set(extra_deps)
        if self.barrier_idx is not None:
            deps.add(self.barrier_idx)
        for k in reads:
            w = self.last_w.get(k)
            if w is not None:
                deps.add(w)
        for k in writes:
            w = self.last_w.get(k)
            if w is not None:
                deps.add(w)
            for r in self.readers.get(k, ()):
                deps.add(r)
        deps.discard(idx)
        self.ops.append(dict(engine=engine, fn=fn, deps=deps, dma=semkey is not None,
                             semkey=semkey, signal=False))
        for k in reads:
            self.readers.setdefault(k, []).append(idx)
        for k in writes:
            self.last_w[k] = idx
            self.readers[k] = []
        if semkey is None:
            self.last_eng[engine] = idx
        else:
            self.dma_since.append(idx)
        return idx

    def dma(self, fn, reads=(), writes=(), semkey=None, queue="sp"):
        return self.op(queue, fn, reads, writes, semkey=semkey)

    def barrier(self):
        if getattr(self, "disabled", False):
            return
        deps = set(self.last_eng.values()) | set(self.dma_since)
        nc = self.nc
        old = self.barrier_idx
        self.barrier_idx = None
        idx = self.op("pool", lambda: nc.gpsimd.nop(), extra_deps=deps | ({old} if old is not None else set()))
        self.barrier_idx = idx
        self.dma_since = []
        self.last_w = {}
        self.readers = {}

    def emit(self, final_engine="sp"):
        ops = self.ops
        for i, o in enumerate(ops):
            nd = set()
            for d in o["deps"]:
                do = ops[d]
                if (not do["dma"]) and (not o["dma"]) and do["engine"] == o["engine"] == "pe":
                    continue
                nd.add(d)
            o["deps"] = nd
            for d in nd:
                ops[d]["signal"] = True
        for o in ops:
            if o["dma"]:
                o["signal"] = True
        cnt = {}
        for o in ops:
            if not o["signal"]:
                continue
            if o["dma"]:
                key = ("dma", o["semkey"])
                cnt[key] = cnt.get(key, 0) + 16
            else:
                key = ("eng", o["engine"])
                cnt[key] = cnt.get(key, 0) + 1
            o["sem"] = key
            o["val"] = cnt[key]
        waited = {}
        nwait = 0
        for i, o in enumerate(ops):
            e = self.eng(o["engine"])
            need = {}
            for d in o["deps"]:
                do = ops[d]
                s = do["sem"]
                need[s] = max(need.get(s, 0), do["val"])
            for s, v in need.items():
                wk = (o["engine"], s)
                if waited.get(wk, 0) >= v:
                    continue
                waited[wk] = v
                e.wait_ge(self.sem(s), v)
                nwait += 1
            ins = o["fn"]()
            if o["signal"]:
                ins.then_inc(self.sem(o["sem"]), 16 if o["dma"] else 1)
        fe = self.eng(final_engine)
        for key, v in cnt.items():
            if key[0] == "dma":
                fe.wait_ge(self.sem(key), v)
        return dict(n_ops=len(ops), n_wait=nwait, n_sems=len(self.sems))


CST = {}
_o = 0
for _n, _w in (("ident", 128), ("ropec", 256), ("ropes", 256), ("smf", 128), ("smb", 128), ("iota", 512),
               ("kvals", 16), ("sgn1", 1), ("sgn2", 1), ("g1T", 8), ("g2T", 8), ("sink", 8), ("dcol", 32),
               ("lamr", 64), ("lami", 64), ("ldt", 64)):
    CST[_n] = (_o, _w)
    _o += _w
NCST = _o
NCSTB = 128 + 1024


def build(debug=False):
    nc = bass.Bass("TRN2", target_bir_lowering=False)
    dt_in = lambda name, shape, dt=F32: nc.dram_tensor(name, shape, dt, kind="ExternalInput").ap()
    x_d = dt_in("x", [L, D])
    win_d = dt_in("w_in", [D, 3328])
    wglu_d = dt_in("w_glu", [512, 512])
    wab_d = dt_in("w_ab", [512, 1024])
    wsb_d = dt_in("w_sb", [512, 1024])
    wout_d = dt_in("w_out", [D, D])
    wfg_d = dt_in("w_fg", [D, DFF])
    wfu_d = dt_in("w_fu", [D, DFF])
    wfd_d = dt_in("w_fd", [DFF, D])
    gf_d = dt_in("gfb", [128, D])
    cst_d = dt_in("cst", [128, NCST])
    cstb_d = dt_in("cstb", [128, NCSTB], BF16)
    B1_d = dt_in("B1", [128, 1024])
    B2_d = dt_in("B2", [128, 1024])
    CT1_d = dt_in("CT1", [128, 512])
    CT2_d = dt_in("CT2", [128, 512])
    jp_d = dt_in("jperm", [128, 128], BF16)
    y_d = nc.dram_tensor("y", [L, D], F32, kind="ExternalOutput").ap()
    x2_d = nc.dram_tensor("x2s", [L, D], F32, kind="ExternalOutput" if debug else "Internal").ap()
    if debug:
        dbg_attn = nc.dram_tensor("dbg_attn", [128, 4 * L], BF16, kind="ExternalOutput").ap()
        dbg_ssm = nc.dram_tensor("dbg_ssm", [128, 4 * L], BF16, kind="ExternalOutput").ap()
        dbg_y = nc.dram_tensor("dbg_y", [128, 4 * 8 * 512], BF16, kind="ExternalOutput").ap()

    st = ExitStack()
    with st:
        ARENA_B = 208896
        arena = st.enter_context(nc.sbuf_tensor("arena", [128, ARENA_B // 2], BF16))
        psum = st.enter_context(nc.psum_tensor("psum", [128, 8, 512], F32))
        P = Prog(nc, st)

        def V(off, shape, dt=BF16, parts=None):
            n = int(np.prod(shape))
            esz = 2 if dt == BF16 else 4
            assert off % 4 == 0
            a = arena[:, off // 2:(off + n * esz) // 2]
            if dt != BF16:
                a = a.bitcast(dt)
            if len(shape) > 1:
                names = "abcd"[:len(shape)]
                kw = {names[i]: shape[i] for i in range(len(shape))}
                a = a.rearrange("p (" + " ".join(names) + ") -> p " + " ".join(names), **kw)
            return a

        def bank(i):
            return psum[:, i, :]

        def bankb(i):
            return psum[:, i, :].bitcast(BF16)

        P_OFF = 0
        XS_OFF = 14336
        STG_OFF = XS_OFF + 10240
        R1 = STG_OFF + 8192
        R2 = R1 + 32768
        R3 = R2 + 65536
        R4 = R3 + 32768
        assert R4 + 45056 == ARENA_B

        cst = V(P_OFF, [NCST], F32)
        assert NCST * 4 <= 6720
        po = P_OFF + 6720
        cstb = V(po, [NCSTB], BF16); po += NCSTB * 2
        ones_b = V(po, [128], BF16); po += 256
        es_t = V(po, [1024], BF16); po += 2048
        ssb = V(po, [96], F32); po += 384
        rsb = V(po, [96], F32); po += 384
        rstd = V(po, [96], F32); po += 384
        sm64 = V(po, [64 * 6], F32); po += 64 * 6 * 4
        esf = V(po, [8], F32); po += 64
        esh = V(po, [8], BF16); po += 64
        esl = V(po, [8], F32); po += 64
        assert po <= XS_OFF, po

        def C(name):
            o, w = CST[name]
            return cst[:, o:o + w]

        ident_f = C("ident")
        ident_b = cstb[:, 0:128]
        maskb = cstb[:, 128:1152]
        xs = [V(XS_OFF, [1024], F32), V(XS_OFF + 4096, [1024], F32)]
        xn = V(XS_OFF + 8192, [1024], BF16)
        stg = [V(STG_OFF, [1024], F32), V(STG_OFF + 4096, [1024], F32)]

        A = nc.scalar
        DV = nc.vector
        PL = nc.gpsimd
        PE = nc.tensor

        def mm(out, lhsT, rhs, start, stop):
            return lambda: PE.matmul(out, lhsT=lhsT, rhs=rhs, start=start, stop=stop)

        def tr(out, in_, ident):
            return lambda: PE.transpose(out, in_, ident)

        def act(out, in_, func, scale=1.0, bias=None, accum=None):
            kw = {}
            if bias is not None:
                kw["bias"] = bias
            if accum is not None:
                kw["accum_out"] = accum
            return lambda: A.activation(out=out, in_=in_, func=func, scale=scale, **kw)

        def tt(eng, out, in0, in1, op):
            return lambda: eng.tensor_tensor(out=out, in0=in0, in1=in1, op=op)

        def ts(eng, out, in0, s1, s2, op0, op1=None):
            if op1 is None:
                return lambda: eng.tensor_scalar(out=out, in0=in0, scalar1=s1, scalar2=None, op0=op0)
            return lambda: eng.tensor_scalar(out=out, in0=in0, scalar1=s1, scalar2=s2, op0=op0, op1=op1)

        def stt(out, in0, scalar, in1, op0, op1):
            return lambda: DV.scalar_tensor_tensor(out=out, in0=in0, scalar=scalar, in1=in1, op0=op0, op1=op1)

        def cp(eng, out, in_):
            if eng is A:
                return lambda: A.copy(out=out, in_=in_)
            return lambda: eng.tensor_copy(out=out, in_=in_)

        def dma(out, in_):
            return lambda: nc.sync.dma_start(out=out, in_=in_)

        stg_n = [0]

        WCOLS = {}

        def load_w(dram, dst, ncols, nk, key, per_chunk_sem=False, per_piece_sem=False, pieces=None):
            WCOLS[key] = ncols
            if pieces is None:
                pieces = [(kc, c0) for kc in range(nk) for c0 in range(0, ncols, 1024)]
            for (kc, c0) in pieces:
                w = min(1024, ncols - c0)
                if per_piece_sem:
                    sk = ("%s_c%d" % (key, c0)) if ncols > 1024 else ("%s_g%d" % (key, kc // 6))
                elif per_chunk_sem:
                    sk = key + str(kc)
                else:
                    sk = key
                P.dma((lambda kc=kc, c0=c0, w=w: PL.dma_start(out=dst[:, kc, c0:c0 + w],
                                                              in_=dram[kc * 128:(kc + 1) * 128, c0:c0 + w])),
                      writes=[(key, kc, c0)], semkey=sk, queue="pool")

        def wk_all(key, nk):
            return [(key, kc, c0) for kc in range(nk) for c0 in range(0, WCOLS[key], 1024)]

        sscol = [0]

        def norm_tile(src_dram, slot, hT_dst, pbank, xkey, gT):
            col = sscol[0] % 96
            sscol[0] += 1
            P.dma(dma(xs[slot][:], src_dram), writes=[("xs", slot)], semkey="xs%d" % slot)
            P.op("act", act(xn[:], xs[slot][:], AF.Square, accum=ssb[:, col:col + 1]),
                 reads=[("xs", slot)], writes=["xn", ("ss", col)])
            P.op("act", act(rsb[:, col:col + 1], ssb[:, col:col + 1], AF.Sqrt, scale=1.0 / D, bias=EPS),
                 reads=[("ss", col)], writes=[("rs", col)])
            P.op("dve", lambda: DV.reciprocal(out=rstd[:, col:col + 1], in_=rsb[:, col:col + 1]),
                 reads=[("rs", col)], writes=[("rstd", col)])
            P.op("dve", ts(DV, xn[:], xs[slot][:], rstd[:, col:col + 1], None, ALU.mult),
                 reads=[("xs", slot), ("rstd", col)], writes=["xn"])
            pb = bankb(pbank)
            for k in range(8):
                P.op("pe", tr(pb[:, k * 128:(k + 1) * 128], xn[:, k * 128:(k + 1) * 128], ident_b),
                     reads=["xn", "cstb"], writes=[("pb", pbank)])
            P.op("dve", tt(DV, hT_dst, pb.rearrange("p (k c) -> p k c", k=8),
                           gT.unsqueeze(2).to_broadcast([128, 8, 128]), ALU.mult),
                 reads=[("pb", pbank), "cst"], writes=[xkey])
            return col

        PHASE = [0]
        KSTOP = int(os.environ.get('KSTOP', '99'))
        P.dma(dma(cst[:], cst_d), writes=["cst"], semkey="c0")
        P.dma(dma(cstb[:], cstb_d), writes=["cstb"], semkey="c1")
        P.op("pool", lambda: PL.memset(ones_b[:], 1.0), writes=["ones"])
        P.op("pool", lambda: PL.memset(es_t[:], 0.0), writes=["es"])
        P.op("act", act(esf[:], C("sink"), AF.Exp), reads=["cst"], writes=["esf"])
        P.op("dve", cp(DV, esh[:], esf[:]), reads=["esf"], writes=["esh"])
        P.op("dve", tt(DV, esl[:], esf[:], esh[:], ALU.subtract), reads=["esf", "esh"], writes=["esl"])
        es_v = es_t.rearrange("p (h q) -> p h q", h=8)
        P.op("dve", cp(DV, es_v[0:1], esh[0:1].unsqueeze(2).to_broadcast([1, 8, 128])), reads=["esh", "es"], writes=["es"])
        P.op("dve", cp(DV, es_v[32:33], esl[32:33].unsqueeze(2).to_broadcast([1, 8, 128])), reads=["esl", "es"], writes=["es"])

        xnb = [None]
        xs4 = [xs[0], xs[1], stg[0], stg[1]]

        def norm_pre(src_dram, slot, xb, do_load=True, do_stat=True):
            if do_load:
                P.dma(dma(xs4[slot][:], src_dram), writes=[("xs", slot)], semkey="xs%d" % slot)
            if not do_stat:
                return
            col = sscol[0] % 96
            sscol[0] += 1
            P.op("act", act(xnb[0][xb][:], xs4[slot][:], AF.Square, accum=ssb[:, col:col + 1]),
                 reads=[("xs", slot)], writes=[("xn", xb), ("ss", col)])
            P.op("act", act(rsb[:, col:col + 1], ssb[:, col:col + 1], AF.Sqrt, scale=1.0 / D, bias=EPS),
                 reads=[("ss", col)], writes=[("rs", col)])
            P.op("dve", lambda: DV.reciprocal(out=rstd[:, col:col + 1], in_=rsb[:, col:col + 1]),
                 reads=[("rs", col)], writes=[("rstd", col)])
            P.op("dve", ts(DV, xnb[0][xb][:], xs4[slot][:], rstd[:, col:col + 1], None, ALU.mult),
                 reads=[("xs", slot), ("rstd", col)], writes=[("xn", xb)])

        def norm_tr(xb, hT_dst, pbank, xkey, gT):
            pb = bankb(pbank)
            for k in range(8):
                P.op("pe", tr(pb[:, k * 128:(k + 1) * 128], xnb[0][xb][:, k * 128:(k + 1) * 128], ident_b),
                     reads=[("xn", xb), "cstb"], writes=[("pb", pbank)])
            P.op("dve", tt(DV, hT_dst, pb.rearrange("p (k c) -> p k c", k=8),
                           gT.unsqueeze(2).to_broadcast([128, 8, 128]), ALU.mult),
                 reads=[("pb", pbank), "cst"], writes=[xkey])

        wA = V(R4, [8, 1280], BF16)
        RA = R4 + 20480
        qtm = [V(RA + i * 1024, [512], BF16) for i in range(2)]
        ktm = [V(RA + 2048 + i * 256, [128], BF16) for i in range(2)]
        rt = [[[V(RA + 2560 + ((pp * 2 + w) * 4 + j) * 256, [64], F32) for j in range(4)] for w in range(2)] for pp in range(2)]
        xnb[0] = [xn, V(RA + 6656, [1024], BF16)]
        hTb = [V(R1, [8, 1024], BF16), V(R1 + 16384, [8, 1024], BF16)]
        QT = V(R2, [NT, 512], BF16)
        KT = V(R2 + 32768, [L], BF16)
        Vd = V(R2 + 40960, [NT, 2, 2, 64], BF16)
        Ucm = V(R2 + 57344, [32, 8, 16], BF16)
        X = V(R3, [32, 512], BF16)
        load_w(win_d[:, 0:1280], wA, 1280, 8, "wA", per_chunk_sem=True)
        ropec = C("ropec").rearrange("p (i j) -> p i j", j=8)
        ropes = C("ropes").rearrange("p (i j) -> p i j", j=8)

        def rope(pv, dv_, bshape, i, pkey, rts, rkp, dkey):
            nh = int(np.prod(bshape[1:-1]))
            cb = ropec[:, i, :]
            sb_ = ropes[:, i, :]
            for _ in range(len(bshape) - 2):
                cb = cb.unsqueeze(1)
                sb_ = sb_.unsqueeze(1)
            cb = cb.to_broadcast(bshape)
            sb_ = sb_.to_broadcast(bshape)
            r1, r2 = pv[..., 0:8], pv[..., 8:16]
            if len(bshape) == 4:
                t = [rts[j][:, 0:nh * 8].rearrange("p (a h d) -> p a h d", a=bshape[1], h=bshape[2]) for j in range(4)]
            else:
                t = [rts[j][:, 0:nh * 8].rearrange("p (h d) -> p h d", h=nh) for j in range(4)]
            rk = [rkp + (j,) for j in range(4)]
            P.op("dve", tt(DV, t[0], r1, cb, ALU.mult), reads=[pkey, "cst"], writes=[rk[0]])
            P.op("dve", tt(DV, t[1], r2, sb_, ALU.mult), reads=[pkey], writes=[rk[1]])
            P.op("dve", tt(DV, t[2], r2, cb, ALU.mult), reads=[pkey], writes=[rk[2]])
            P.op("dve", tt(DV, t[3], r1, sb_, ALU.mult), reads=[pkey], writes=[rk[3]])
            P.op("dve", tt(DV, dv_[..., 0:8], t[0], t[1], ALU.subtract), reads=[rk[0], rk[1]], writes=[dkey])
            P.op("dve", tt(DV, dv_[..., 8:16], t[2], t[3], ALU.add), reads=[rk[2], rk[3]], writes=[dkey])
            if len(bshape) == 4:
                for a in range(bshape[1]):
                    P.op("act", cp(A, dv_[:, a, :, 16:64], pv[:, a, :, 16:64]), reads=[pkey], writes=[dkey])
            else:
                P.op("act", cp(A, dv_[..., 16:64], pv[..., 16:64]), reads=[pkey], writes=[dkey])

        def a_pre(i):
            norm_pre(x_d[i * 128:(i + 1) * 128, :], i % 4, i % 2)

        def a_tr(i):
            blk, s8 = i // 8, i % 8
            norm_tr(i % 2, hTb[blk % 2][:, :, s8 * 128:(s8 + 1) * 128], 0, ("hT", blk % 2, s8), C("g1T"))

        def a_mm(i):
            blk, s8 = i // 8, i % 8
            hb = hTb[blk % 2]
            bkv = 1 if i % 2 else 4
            pq, pkv = bank(2 + i % 2), bank(bkv)
            for k in range(8):
                lhs = hb[:, k, s8 * 128:(s8 + 1) * 128]
                P.op("pe", mm(pq, lhs, wA[:, k, 0:512], k == 0, k == 7),
                     reads=[("hT", blk % 2, s8), ("wA", k, 0), ("wA", k, 1024)], writes=[("pb", 2 + i % 2)])
                P.op("pe", mm(pkv[:, 0:256], lhs, wA[:, k, 512:768], k == 0, k == 7),
                     reads=[("hT", blk % 2, s8), ("wA", k, 0), ("wA", k, 1024)], writes=[("pb", bkv)])

        def a_rope(i):
            pp = i % 2
            bkv = 1 if i % 2 else 4
            pq, pkv = bank(2 + i % 2), bank(bkv)
            rope(pq.rearrange("p (a j d) -> p a j d", a=2, j=4),
                 qtm[pp].rearrange("p (j a d) -> p a j d", j=4, a=2), [128, 2, 4, 8], i,
                 ("pb", 2 + i % 2), rt[pp][0], ("rt", pp, 0), ("qtm", pp))
            rope(pkv[:, 0:128].rearrange("p (h d) -> p h d", h=2), ktm[pp].rearrange("p (h d) -> p h d", h=2),
                 [128, 2, 8], i, ("pb", bkv), rt[pp][1], ("rt", pp, 1), ("ktm", pp))
            P.op("act", cp(A, Vd[:, i], pkv[:, 128:256].rearrange("p (h d) -> p h d", h=2).unsqueeze(2)
                           .to_broadcast([128, 2, 2, 64])),
                 reads=[("pb", bkv)], writes=[("Vd", i)])

        def a_qtr(i):
            pp = i % 2
            pQ = bankb(5)
            for j in range(4):
                P.op("pe", tr(pQ[:, j * 128:(j + 1) * 128], qtm[pp][:, j * 128:(j + 1) * 128], ident_b),
                     reads=[("qtm", pp), "cstb"], writes=[("pb", 5)])
            P.op("pe", tr(pQ[:, 512:640], ktm[pp][:], ident_b), reads=[("ktm", pp)], writes=[("pb", 5)])
            P.op("dve", cp(DV, QT[:, i, :], pQ[:, 0:512]), reads=[("pb", 5)], writes=[("QT", i)])
            P.op("act", cp(A, KT[:, i * 128:(i + 1) * 128], pQ[:, 512:640]), reads=[("pb", 5)], writes=[("KT", i)])

        def a_u(blk, s):
            hb = hTb[blk % 2]
            pu = bank(6 + s % 2)
            for k in range(8):
                P.op("pe", mm(pu, hb[:, k, s:1024:8], wA[:, k, 768:1280], k == 0, k == 7),
                     reads=[("hT", blk % 2, j) for j in range(8)] + [("wA", k, 0), ("wA", k, 1024)], writes=[("pb", 6 + s % 2)])
            puv = pu.rearrange("p (g h) -> p g h", g=32)
            if s % 2 == 0:
                P.op("act", cp(A, Ucm[:, :, s, :], puv), reads=[("pb", 6 + s % 2)], writes=[("Ucm", s)])
            else:
                P.op("dve", cp(DV, Ucm[:, :, s, :], puv), reads=[("pb", 6 + s % 2)], writes=[("Ucm", s)])

        def a_xt(blk):
            for g4 in range(4):
                pX = bankb(5)
                for gi in range(8):
                    g = g4 * 8 + gi
                    P.op("pe", tr(pX[:, gi * 128:(gi + 1) * 128], Ucm[:, g].rearrange("p s h -> p (s h)"), ident_b),
                         reads=[("Ucm", s) for s in range(8)], writes=[("pb", 5)])
                P.op("act" if g4 % 2 else "dve",
                     cp(A if g4 % 2 else DV, X[:, g4 * 8:(g4 + 1) * 8, blk * 128:(blk + 1) * 128],
                        pX.rearrange("p (g c) -> p g c", g=8)),
                     reads=[("pb", 5)], writes=[("X", g4)])

        a_pre(0)
        a_tr(0)
        a_pre(1)
        for i in range(NT):
            a_mm(i)
            if i >= 8:
                a_u(i // 8 - 1, i % 8)
            if i + 1 < NT:
                a_tr(i + 1)
            if i + 2 < NT:
                a_pre(i + 2)
            a_rope(i)
            if i >= 1:
                a_qtr(i - 1)
            if i >= 8 and i % 8 == 7:
                a_xt(i // 8 - 1)
        a_qtr(NT - 1)
        for s_ in range(8):
            a_u(3, s_)
        a_xt(3)
        P.barrier()
        PHASE[0] += 1
        if PHASE[0] >= KSTOP:
            P.disabled = True

        attnT = V(R1, [4, L], BF16)
        Pt = [[V(R4 + (b * 3 + d) * 1024, [512], BF16) for d in range(3)] for b in range(2)]
        rD = [V(R4 + 6144 + i * 2048, [512], F32) for i in range(2)]
        lnD = [V(R4 + 10240 + i * 2048, [512], F32) for i in range(2)]
        es_k = es_t.rearrange("p (k q) -> p k q", k=2)
        mb = maskb.rearrange("p (d q) -> p d q", d=2)
        its = [(n, kvh) for n in range(NT) for kvh in range(2)]
        sc_n = [0]

        def att_scores(it):
            n, kvh = its[it]
            pr = slice(kvh * 64, (kvh + 1) * 64)
            for d in (-1, 0, 1):
                if not (0 <= n + d < NT):
                    continue
                bk = sc_n[0] % 4
                sc_n[0] += 1
                pS = bank(bk)
                P.op("pe", mm(pS, KT[pr, (n + d) * 128:(n + d + 1) * 128], QT[pr, n, :], True, True),
                     reads=[], writes=[("pb", bk)])
                P.op("act", act(Pt[it % 2][d + 1][:], pS, AF.Exp, scale=0.125),
                     reads=[("pb", bk)], writes=[("Pt", it % 2, d + 1)])
                if d != 0:
                    P.op("pool", tt(PL, Pt[it % 2][d + 1][:], Pt[it % 2][d + 1][:], mb[:, 0 if d < 0 else 1, :], ALU.mult),
                         reads=[("Pt", it % 2, d + 1)], writes=[("Pt", it % 2, d + 1)])

        def att_out(it):
            n, kvh = its[it]
            dl = [d for d in (-1, 0, 1) if 0 <= n + d < NT]
            bo, bd = 4 + it % 2, 6 + it % 2
            pO, pD = bank(bo), bank(bd)
            for ii, d in enumerate(dl):
                P.op("pe", mm(pD, ones_b[:], Pt[it % 2][d + 1][:], ii == 0, False),
                     reads=[("Pt", it % 2, d + 1)], writes=[("pb", bd)])
            P.op("pe", mm(pD, ones_b[0:33, :], es_k[0:33, kvh, :], False, True), reads=[], writes=[("pb", bd)])
            for ii, d in enumerate(dl):
                P.op("pe", mm(pO, Vd[:, n + d, kvh].rearrange("p a b -> p (a b)"), Pt[it % 2][d + 1][:], ii == 0, ii == len(dl) - 1),
                     reads=[("Pt", it % 2, d + 1)], writes=[("pb", bo)])
            if it % 3 != 2:
                P.op("act", act(lnD[it % 2][:], pD, AF.Ln), reads=[("pb", bd)], writes=[("lnD", it % 2)])
                P.op("act", act(rD[it % 2][:], lnD[it % 2][:], AF.Exp, scale=-1.0), reads=[("lnD", it % 2)], writes=[("rD", it % 2)])
            else:
                P.op("dve", lambda pD=pD, it=it: DV.reciprocal(out=rD[it % 2][:], in_=pD), reads=[("pb", bd)], writes=[("rD", it % 2)])
            pOv = pO.rearrange("p (j q) -> p j q", j=4)
            rDv = rD[it % 2].rearrange("p (j q) -> p j q", j=4)
            for half in range(2):
                rows = slice(half * 64, (half + 1) * 64)
                P.op("dve", tt(DV, attnT[rows, kvh * 2:kvh * 2 + 2, n * 128:(n + 1) * 128],
                               pOv[rows, half:4:2, :], rDv[rows, half:4:2, :], ALU.mult),
                     reads=[("pb", bo), ("rD", it % 2)], writes=[("attnT", n, kvh, half)])

        att_scores(0)
        for it in range(len(its)):
            if it + 1 < len(its):
                att_scores(it + 1)
            att_out(it)
        if debug:
            P.dma(dma(dbg_attn, attnT.rearrange("p a b -> p (a b)")), reads=[("attnT", n, k_, h_) for n in range(NT) for k_ in range(2) for h_ in range(2)], semkey="dbg0")
        P.barrier()
        PHASE[0] += 1
        if PHASE[0] >= KSTOP:
            P.disabled = True

        B1t = V(R4, [64, 16], F32)
        B2t = V(R4 + 4096, [64, 16], F32)
        CT1t = V(R4 + 8192, [32, 16], F32)
        CT2t = V(R4 + 10240, [32, 16], F32)
        TT = [V(R4 + 12288 + i * 4096, [64, 16], F32) for i in range(6)]
        T3i = V(R4 + 12288 + 2 * 4096, [64, 16], I32)
        BB1 = V(XS_OFF, [64, 16], F32)
        BB2 = V(XS_OFF + 4096, [64, 16], F32)
        pa = V(STG_OFF, [8, 8, 16], F32)
        pbt = V(STG_OFF + 4096, [8, 8, 16], F32)
        WsT = [V(R2 + d * 8192, [32, 128], BF16) for d in range(2)]
        Wy = [V(R2 + 16384 + d * 8192, [32, 128], BF16) for d in range(2)]
        Wyp = [V(R2 + 32768 + d * 8192, [32, 128], BF16) for d in range(2)]
        Rr = [V(R2 + 49152 + d * 8192, [32, 128], BF16) for d in range(2)]
        for (dst, src, nm) in ((B1t, B1_d, "B1"), (B2t, B2_d, "B2")):
            P.dma(dma(dst.rearrange("p a b -> p (a b)"), src), writes=[nm], semkey="p" + nm)
        for (dst, src, nm) in ((CT1t, CT1_d, "CT1"), (CT2t, CT2_d, "CT2")):
            P.dma(dma(dst.rearrange("p a b -> p (a b)"), src), writes=[nm], semkey="p" + nm)
        dtv, arv, atv, rdec, f8, tmp64 = [sm64[:, i * 64:(i + 1) * 64] for i in range(6)]
        lamr, lami = C("lamr"), C("lami")
        P.op("act", act(dtv, C("ldt"), AF.Exp), reads=["cst"], writes=["dt"])
        P.op("dve", tt(DV, arv, lamr, dtv, ALU.mult), reads=["dt"], writes=["ar"])
        P.op("dve", stt(atv, lami, float(1 / (2 * math.pi)), dtv, ALU.mult, ALU.mult), reads=["dt"], writes=["at"])
        kv = C("kvals")
        b3 = lambda a: a.unsqueeze(2).to_broadcast([128, 64, 16])
        kb = kv.unsqueeze(1).to_broadcast([128, 64, 16])
        P.op("dve", tt(DV, TT[0], b3(arv), kb, ALU.mult), reads=["ar"], writes=["T0"])
        P.op("act", act(TT[0], TT[0], AF.Exp), reads=["T0"], writes=["T0"])
        P.op("dve", tt(DV, TT[1], b3(atv), kb, ALU.mult), reads=["at"], writes=["T1"])
        P.op("dve", ts(DV, T3i, TT[1], 1.0, None, ALU.mult), reads=["T1"], writes=["T2"])
        P.op("dve", cp(DV, TT[3], T3i), reads=["T2"], writes=["T3"])
        P.op("dve", tt(DV, TT[1], TT[1], TT[3], ALU.subtract), reads=["T1", "T3"], writes=["T1"])
        P.op("act", act(TT[3], TT[1], AF.Abs), reads=["T1"], writes=["T3"])
        P.op("act", act(TT[2], TT[1], AF.Sin, scale=TWO_PI_S), reads=["T1", "T2"], writes=["T2"])
        P.op("act", act(TT[3], TT[3], AF.Sin, scale=-TWO_PI_S, bias=HALF_PI), reads=["T3"], writes=["T3"])
        P.op("dve", tt(DV, TT[3], TT[3], TT[0], ALU.mult), reads=["T3", "T0"], writes=["T3"])
        P.op("dve", tt(DV, TT[2], TT[2], TT[0], ALU.mult), reads=["T2", "T0"], writes=["T2"])
        Er, Ei = TT[3], TT[2]
        P.op("dve", cp(DV, rdec, TT[0][:, :, 15]), reads=["T0"], writes=["rdec"])
        P.op("dve", cp(DV, f8, TT[1][:, :, 15]), reads=["T1"], writes=["f8"])

        s6 = V(R4 + 36864, [64 * 8], F32)
        xr, den, t1, t2, ber, bei, beS1, beS2 = [s6[:, i * 64:(i + 1) * 64] for i in range(8)]
        E1r, E1i = Er[:, :, 8], Ei[:, :, 8]
        P.op("dve", ts(DV, xr, E1r, -1.0, None, ALU.add), reads=["T3"], writes=["xr"])
        P.op("dve", tt(DV, den, lamr, lamr, ALU.mult), reads=["cst"], writes=["den"])
        P.op("dve", tt(DV, t1, lami, lami, ALU.mult), reads=["cst"], writes=["t1"])
        P.op("dve", tt(DV, den, den, t1, ALU.add), reads=["den", "t1"], writes=["den"])
        P.op("dve", lambda: DV.reciprocal(out=den, in_=den), reads=["den"], writes=["den"])
        P.op("dve", tt(DV, t1, xr, lamr, ALU.mult), reads=["xr", "t1"], writes=["t1"])
        P.op("dve", tt(DV, t2, E1i, lami, ALU.mult), reads=["T2"], writes=["t2"])
        P.op("dve", tt(DV, t1, t1, t2, ALU.add), reads=["t1", "t2"], writes=["t1"])
        P.op("dve", tt(DV, ber, t1, den, ALU.mult), reads=["t1", "den"], writes=["ber"])
        P.op("dve", tt(DV, t1, E1i, lamr, ALU.mult), reads=["T2", "ber"], writes=["t1"])
        P.op("dve", tt(DV, t2, xr, lami, ALU.mult), reads=["xr", "t2"], writes=["t2"])
        P.op("dve", tt(DV, t1, t1, t2, ALU.subtract), reads=["t1", "t2"], writes=["t1"])
        P.op("dve", tt(DV, bei, t1, den, ALU.mult), reads=["t1", "den"], writes=["bei"])
        sg1, sg2 = C("sgn1"), C("sgn2")
        P.op("dve", ts(DV, beS1, bei, sg1, None, ALU.mult), reads=["bei"], writes=["beS1"])
        P.op("dve", ts(DV, beS2, bei, sg2, None, ALU.mult), reads=["bei"], writes=["beS2"])
        bh = lambda a: a.unsqueeze(2).to_broadcast([128, 64, 16])
        P.op("dve", tt(DV, BB1, B1t, bh(ber), ALU.mult), reads=["B1", "ber"], writes=["BB1"])
        P.op("dve", tt(DV, TT[4], B2t, bh(beS1), ALU.mult), reads=["B2", "beS1"], writes=["T4"])
        P.op("dve", tt(DV, BB1, BB1, TT[4], ALU.add), reads=["BB1", "T4"], writes=["BB1"])
        P.op("dve", tt(DV, BB2, B2t, bh(ber), ALU.mult), reads=["B2", "ber"], writes=["BB2"])
        P.op("dve", tt(DV, TT[4], B1t, bh(beS2), ALU.mult), reads=["B1", "beS2", "BB1"], writes=["T4"])
        P.op("dve", tt(DV, BB2, BB2, TT[4], ALU.add), reads=["BB2", "T4"], writes=["BB2"])
        P.op("dve", ts(DV, TT[0], Er, sg2, None, ALU.mult), reads=["T3", "rdec"], writes=["T0"])
        P.op("dve", ts(DV, TT[1], Ei, sg1, None, ALU.mult), reads=["T2", "f8"], writes=["T1"])
        P.op("dve", ts(DV, TT[4], Ei, -1.0, None, ALU.mult), reads=["T2", "BB2"], writes=["T4"])
        P.op("dve", ts(DV, TT[5], Er, -1.0, None, ALU.mult), reads=["T3"], writes=["T5"])
        ErA, EiS, EiN, ErN = TT[0], TT[1], TT[4], TT[5]

        def ksel(tab, dg0, k0, rev):
            v = tab[:, dg0:dg0 + 8, :]
            v = v[:, :, k0:k0 - 8 if k0 - 8 >= 0 else None:-1] if rev else v[:, :, k0:k0 + 8]
            return v.unsqueeze(3).to_broadcast([128, 8, 8, 16])

        def build_tab(dst, d, g0, tA, tB, cA, cB, k0, rev, is_b):
            dg0 = d * 32 + g0
            if is_b:
                ca = cA[:, dg0:dg0 + 8, :].unsqueeze(2).to_broadcast([128, 8, 8, 16])
                cb_ = cB[:, dg0:dg0 + 8, :].unsqueeze(2).to_broadcast([128, 8, 8, 16])
            else:
                ca = cA[:, g0:g0 + 8, :].unsqueeze(2).to_broadcast([128, 8, 8, 16])
                cb_ = cB[:, g0:g0 + 8, :].unsqueeze(2).to_broadcast([128, 8, 8, 16])
            P.op("dve", tt(DV, pa, ksel(tA, dg0, k0, rev), ca, ALU.mult), reads=["T0", "T1", "T4", "T5", "T2", "T3", "BB1", "BB2", "CT1", "CT2"], writes=["pa"])
            P.op("dve", tt(DV, pbt, ksel(tB, dg0, k0, rev), cb_, ALU.mult), reads=["T0", "T1", "T4", "T5", "T2", "T3", "BB1", "BB2", "CT1", "CT2"], writes=["pbt"])
            P.op("dve", tt(DV, dst[:, g0:g0 + 8, :].rearrange("p g (s h) -> p g s h", s=8), pa, pbt, ALU.add),
                 reads=["pa", "pbt"], writes=[("tabs", d, g0, id(dst))])

        MgA = V(R4, [32, 128], BF16)
        TL = R4 + 38912
        Mt = V(TL, [128], F32)
        Mt2 = V(TL + 512, [128], F32)
        pMs = [V(TL + 1024 + i * 512, [128], F32) for i in range(2)]
        Jt = V(TL + 2048, [128], BF16)
        P.dma(dma(Jt[:], jp_d), writes=["Jt"], semkey="jp")
        smf, smb = C("smf"), C("smb")
        dcol = C("dcol")
        P.op("pool", lambda: PL.nop(), reads=["BB1", "BB2"], writes=["MgA"])

        def prepass(g):
            bb = 2 if g % 2 == 0 else 4
            for d in range(2):
                pW = bankb(bb + d)[:, 512:640]
                pM = bank(bb + d)
                tk_ = [("tabs", d, (g // 8) * 8, id(WsT[d])), ("tabs", d, (g // 8) * 8, id(Rr[d]))]
                P.op("pe", tr(pW, WsT[d][:, g, :], ident_b), reads=tk_, writes=[("pb", bb + d)])
                P.op("pe", mm(pM[:, 0:128], WsT[d][:, g, :], Rr[d][:, g, :], True, True), reads=tk_, writes=[("pb", bb + d)])
                P.op("act", cp(A, WsT[d][:, g, :], pW), reads=[("pb", bb + d)], writes=[("Ws", d, g)])
                P.op("act", cp(A, Rr[d][:, g, 0:64], pW[:, 64:128]), reads=[("pb", bb + d)], writes=[("Wsp", d, g)])
                P.op("act", act(Rr[d][:, g, 64:128], pW[:, 0:64], AF.Copy, scale=-1.0), reads=[("pb", bb + d)], writes=[("Wsp", d, g)])
                P.op("act", cp(A, pMs[d][:], pM[:, 0:128]), reads=[("pb", bb + d)], writes=[("pMs", d)])
            P.op("dve", tt(DV, Mt[:], pMs[0][:], smf, ALU.mult), reads=[("pMs", 0)], writes=["Mt"])
            P.op("dve", tt(DV, Mt2[:], pMs[1][:], smb, ALU.mult), reads=[("pMs", 1)], writes=["Mt2"])
            P.op("dve", tt(DV, Mt[:], Mt[:], Mt2[:], ALU.add), reads=["Mt", "Mt2"], writes=["Mt"])
            P.op("dve", stt(MgA[:, g, :], ident_f, dcol[:, g:g + 1], Mt[:], ALU.mult, ALU.add), reads=["Mt", "MgA"], writes=[("Mg", g)])

        for g0 in range(0, 32, 8):
            for d in range(2):
                build_tab(WsT[d], d, g0, Er, EiS, BB1, BB2, 14 if d == 0 else 7, d == 0, True)
                build_tab(Wy[d], d, g0, ErA, EiN, CT1t, CT2t, 8 if d == 0 else 15, d == 1, False)
                for hh_ in range(2):
                    bj = 6 + hh_
                    gsl = slice(g0 + 4 * hh_, g0 + 4 * hh_ + 4)
                    P.op("pe", mm(bank(bj), Jt[:], Wy[d][:, gsl, :].rearrange("p g c -> p (g c)"), True, True),
                         reads=[("tabs", d, g0, id(Wy[d])), "Jt"], writes=[("pb", bj)])
                    P.op("act", cp(A, Wyp[d][:, gsl, :].rearrange("p g c -> p (g c)"), bank(bj)),
                         reads=[("pb", bj)], writes=[("tabs", d, g0, id(Wyp[d]), hh_)])
                build_tab(Rr[d], d, g0, ErA, EiN, CT1t, CT2t, 0 if d == 0 else 7, d == 1, False)
            if g0 >= 8:
                for g in range(g0 - 8, g0):
                    prepass(g)
        for g in range(24, 32):
            prepass(g)
        P.barrier()
        PHASE[0] += 1
        if PHASE[0] >= KSTOP:
            P.disabled = True

        Wsg = lambda d, g: WsT[d][:, g, :]
        Wspg = lambda d, g: Rr[d][:, g, :]
        GcS = [V(R4 + 8192 + d * 1024, [512], BF16) for d in range(2)]
        GsS = [V(R4 + 10240 + d * 1024, [512], BF16) for d in range(2)]
        WB = R4 + 12288
        sTb = [[V(WB + (pp * 2 + d) * 8192, [512], F32) for d in range(2)] for pp in range(2)]
        cTb = [[V(WB + 2048 + (pp * 2 + d) * 8192, [512], F32) for d in range(2)] for pp in range(2)]
        Aab = [[V(WB + 4096 + (pp * 2 + d) * 8192, [512], F32) for d in range(2)] for pp in range(2)]
        Bbb = [[V(WB + 6144 + (pp * 2 + d) * 8192, [512], F32) for d in range(2)] for pp in range(2)]
        assert WB + 32768 <= ARENA_B
        ycm = X
        iota = C("iota")
        for d in range(2):
            P.op("pool", lambda d=d: PL.memset(GcS[d][:], 0.0), writes=[("Gc", d)])
            P.op("pool", lambda d=d: PL.memset(GsS[d][:], 0.0), writes=[("Gs", d)])

        def stage_Ta(g, d):
            dg = d * 32 + g
            pp = g % 2
            sT, cT = sTb[pp][d], cTb[pp][d]
            P.op("act", act(cT[:].bitcast(I32), iota, AF.Identity, scale=f8[:, dg:dg + 1]), reads=[], writes=[("cT", pp, d)])
            P.op("act", act(sT[:], cT[:].bitcast(I32), AF.Identity), reads=[("cT", pp, d)], writes=[("sT", pp, d)])

        def stage_Tb(g, d):
            dg = d * 32 + g
            pp = g % 2
            sT, cT = sTb[pp][d], cTb[pp][d]
            P.op("dve", stt(sT[:], iota, f8[:, dg:dg + 1], sT[:], ALU.mult, ALU.subtract), reads=[("sT", pp, d)], writes=[("sT", pp, d)])
            P.op("act", act(cT[:], sT[:], AF.Abs), reads=[("sT", pp, d)], writes=[("cT", pp, d)])
            P.op("act", act(sT[:], sT[:], AF.Sin, scale=TWO_PI_S), reads=[("sT", pp, d)], writes=[("sT", pp, d)])
            P.op("act", act(cT[:], cT[:], AF.Sin, scale=-TWO_PI_S, bias=HALF_PI), reads=[("cT", pp, d)], writes=[("cT", pp, d)])

        def stage_S(g, d):
            pS, pSp = bank(4 + 2 * d), bank(5 + 2 * d)
            P.op("pe", mm(pS, Wsg(d, g), X[:, g, :], True, True), reads=[("X", g)], writes=[("pb", 4 + 2 * d)])
            P.op("pe", mm(pSp, Wspg(d, g), X[:, g, :], True, True), reads=[("X", g)], writes=[("pb", 5 + 2 * d)])

        def stage_M1(g, d):
            pp = g % 2
            pS, pSp = bank(4 + 2 * d), bank(5 + 2 * d)
            sT, cT, Aa, Bb = sTb[pp][d], cTb[pp][d], Aab[pp][d], Bbb[pp][d]
            rv = (lambda a: a[:, ::-1]) if d == 1 else (lambda a: a)
            P.op("dve", tt(DV, Aa[:], pS, rv(cT[:]), ALU.mult), reads=[("pb", 4 + 2 * d), ("cT", pp, d)], writes=[("Aa", pp, d)])
            P.op("dve", tt(DV, Bb[:], pSp, rv(sT[:]), ALU.mult), reads=[("pb", 5 + 2 * d), ("sT", pp, d)], writes=[("Bb", pp, d)])
            if d == 0:
                P.op("dve", tt(DV, Aa[:], Aa[:], Bb[:], ALU.add), reads=[("Aa", pp, d), ("Bb", pp, d)], writes=[("Aa", pp, d)])
            else:
                P.op("pool", tt(PL, Aa[:], Aa[:], Bb[:], ALU.add), reads=[("Aa", pp, d), ("Bb", pp, d)], writes=[("Aa", pp, d)])

        def stage_M2(g, d):
            dg = d * 32 + g
            pp = g % 2
            sT, cT, Aa, Bb = sTb[pp][d], cTb[pp][d], Aab[pp][d], Bbb[pp][d]
            Gt = Bb
            rv = (lambda a: a[:, ::-1]) if d == 1 else (lambda a: a)
            P.op("dve", lambda: DV.tensor_tensor_scan(
                out=rv(Gt[:]), data0=rdec[:, dg:dg + 1].to_broadcast([128, 512]), data1=rv(Aa[:]),
                initial=0.0, op0=ALU.mult, op1=ALU.add), reads=[("Aa", pp, d), ("Bb", pp, d)], writes=[("Bb", pp, d)])
            if d == 0:
                P.op("pool", tt(PL, GcS[d][:, 1:512], Gt[:, 0:511], cT[:, 0:511], ALU.mult),
                     reads=[("Bb", pp, d), ("cT", pp, d)], writes=[("Gc", d)])
                P.op("pool", tt(PL, GsS[d][:, 1:512], Gt[:, 0:511], sT[:, 0:511], ALU.mult),
                     reads=[("Bb", pp, d), ("sT", pp, d)], writes=[("Gs", d)])
            else:
                P.op("pool", tt(PL, GcS[d][:, 0:511], Gt[:, 1:512], cT[:, 510::-1], ALU.mult),
                     reads=[("Bb", pp, d), ("cT", pp, d)], writes=[("Gc", d)])
                P.op("pool", tt(PL, GsS[d][:, 0:511], Gt[:, 1:512], sT[:, 510::-1], ALU.mult),
                     reads=[("Bb", pp, d), ("sT", pp, d)], writes=[("Gs", d)])

        def stage_O(g):
            pY = bank(g % 2)
            for cb in range(4):
                cs = slice(cb * 128, (cb + 1) * 128)
                o = pY[:, cb * 128:(cb + 1) * 128]
                P.op("pe", mm(o, X[:, g, cs], MgA[:, g, :], True, False), reads=[("X", g)], writes=[("pb", g % 2)])
                for d in range(2):
                    P.op("pe", mm(o, GcS[d][:, cs], Wy[d][:, g, :], False, False), reads=[("Gc", d)], writes=[("pb", g % 2)])
                    P.op("pe", mm(o, GsS[d][:, cs], Wyp[d][:, g, :], False, d == 1), reads=[("Gs", d)], writes=[("pb", g % 2)])

        def stage_E(g):
            P.op("act", cp(A, ycm[:, g, :], bank(g % 2)), reads=[("pb", g % 2)], writes=[("X", g)])

        for d in range(2):
            stage_Ta(0, d)
            stage_Tb(0, d)
        for d in range(2):
            stage_S(0, d)
        for g in range(32):
            if g + 1 < 32:
                for d in range(2):
                    stage_Ta(g + 1, d)
            stage_M1(g, 0)
            stage_M1(g, 1)
            if g + 1 < 32:
                for d in range(2):
                    stage_Tb(g + 1, d)
            stage_M2(g, 0)
            stage_M2(g, 1)
            if g + 1 < 32:
                for d in range(2):
                    stage_S(g + 1, d)
            stage_O(g)
            if g >= 1:
                stage_E(g - 1)
        stage_E(31)
        P.barrier()
        PHASE[0] += 1
        if PHASE[0] >= KSTOP:
            P.disabled = True

        yT = V(R2, [4, L], BF16)
        ycm2 = [V(XS_OFF + b * 8192, [8, 512], BF16) for b in range(2)] + \
               [V(R4 + 8192 + b * 8192, [8, 512], BF16) for b in range(2)]
        yv5 = ycm.rearrange("p g (b t h) -> p g b t h", b=4, t=8)
        for blk in range(4):
            for t in range(8):
                P.op("act", act(ycm2[blk][:, t, :].rearrange("p (g h) -> p g h", g=32), yv5[:, :, blk, t, :], AF.Gelu_apprx_tanh),
                     reads=[], writes=[("ycm", blk, t)])
        P.op("pool", lambda: PL.nop(), reads=[("ycm", blk, t) for blk in range(4) for t in range(8)])
        wg = V(R3, [8, 2048], BF16)
        wab = V(R2 + 32768, [4, 1024], BF16)
        wsb = V(R2 + 40960, [4, 1024], BF16)
        wout = V(R2 + 49152, [8, 1024], BF16)
        load_w(win_d[:, 1280:3328], wg, 2048, 8, "wg")
        load_w(wab_d, wab, 1024, 4, "wab")
        load_w(wsb_d, wsb, 1024, 4, "wsb")
        load_w(wout_d, wout, 1024, 8, "wout")
        wglu = V(R4, [4, 512], BF16)
        sgl = [V(R4 + 4096 + i * 1024, [512], BF16) for i in range(4)]
        load_w(wglu_d, wglu, 512, 4, "wglu")
        for blk in range(4):
            for q4 in range(4):
                pZ = bankb((blk * 4 + q4) % 2)
                for t in range(8):
                    P.op("pe", tr(pZ[:, t * 128:(t + 1) * 128], ycm2[blk][:, t, q4 * 128:(q4 + 1) * 128], ident_b),
                         reads=[("ycm", blk, t)], writes=[("pb", (blk * 4 + q4) % 2)])
                P.op("dve" if q4 % 2 else "act",
                     cp(DV if q4 % 2 else A, yT[:, q4, blk * 1024:(blk + 1) * 1024].rearrange("p (c t) -> p t c", t=8),
                        pZ.rearrange("p (t c) -> p t c", t=8)),
                     reads=[("pb", (blk * 4 + q4) % 2)], writes=[("yT", blk)])
        for n5 in range(8):
            cs = slice(n5 * 512, (n5 + 1) * 512)
            for mo in range(4):
                pg = bank(2 + mo)
                for ki in range(4):
                    P.op("pe", mm(pg, wglu[:, ki, mo * 128:(mo + 1) * 128], yT[:, ki, cs], ki == 0, ki == 3),
                         reads=wk_all("wglu", 4) + [("yT", n5 // 2)], writes=[("pb", 2 + mo)])
                P.op("act", act(sgl[mo][:], pg, AF.Sigmoid), reads=[("pb", 2 + mo)], writes=[("sgl", mo)])
            for mo in range(4):
                P.op("dve", tt(DV, yT[:, mo, cs], yT[:, mo, cs], sgl[mo][:], ALU.mult),
                     reads=[("sgl", mo), ("yT", n5 // 2)], writes=[("yT", n5 // 2)])
        if debug:
            P.dma(dma(dbg_ssm, yT.rearrange("p a b -> p (a b)")), reads=[("yT", b) for b in range(4)], semkey="dbg1")
        P.barrier()
        PHASE[0] += 1
        if PHASE[0] >= KSTOP:
            P.disabled = True

        TB = 256
        hT5 = [V(R4 + i * 4096, [8, TB], BF16) for i in range(2)]
        sgt = [V(R4 + 8192 + i * 512, [TB], BF16) for i in range(2)]
        t12 = [V(R4 + 9216 + i * 1024, [TB], F32) for i in range(2)]
        mgT = V(R4 + 11264, [8, TB], BF16)
        x2t = [V(R4 + 15360 + i * 4096, [1024], F32) for i in range(2)]
        xnb[0] = [xn, V(R4 + 23552, [1024], BF16)]
        NT2 = L // TB

        def b1_pre(n2, **kw):
            for sub in range(2):
                i = n2 * 2 + sub
                norm_pre(x_d[i * 128:(i + 1) * 128, :], (n2 % 2) * 2 + sub, sub, **kw)

        def b1_tr(n2):
            for sub in range(2):
                norm_tr(sub, hT5[n2 % 2][:, :, sub * 128:(sub + 1) * 128], 6 + sub, ("hT5", n2 % 2, sub), C("g1T"))

        b1_pre(0)
        b1_tr(0)
        for n2 in range(NT2):
            cs = slice(n2 * TB, (n2 + 1) * TB)
            hh = hT5[n2 % 2]
            hk = [("hT5", n2 % 2, 0), ("hT5", n2 % 2, 1)]
            for mo in range(8):
                for br in range(2):
                    pg = bank(br)
                    for k in range(8):
                        P.op("pe", mm(pg[:, 0:TB], wg[:, k, (br * 8 + mo) * 128:(br * 8 + mo + 1) * 128], hh[:, k, :], k == 0, k == 7),
                             reads=hk + wk_all("wg", 8), writes=[("pb", br)])
                    P.op("act", act(sgt[br][:], pg[:, 0:TB], AF.Sigmoid), reads=[("pb", br)], writes=[("sgt", br)])
                    pb_ = bank(2 + br)
                    wbr, src, wkey = (wab, attnT, "wab") if br == 0 else (wsb, yT, "wsb")
                    for ki in range(4):
                        P.op("pe", mm(pb_[:, 0:TB], wbr[:, ki, mo * 128:(mo + 1) * 128], src[:, ki, cs], ki == 0, ki == 3),
                             reads=wk_all(wkey, 4), writes=[("pb", 2 + br)])
                    P.op("dve", tt(DV, t12[br][:], pb_[:, 0:TB], sgt[br][:], ALU.mult),
                         reads=[("pb", 2 + br), ("sgt", br)], writes=[("t12", br)])
                P.op("dve", tt(DV, mgT[:, mo, :], t12[0][:], t12[1][:], ALU.add),
                     reads=[("t12", 0), ("t12", 1)], writes=[("mgT", mo)])
                if n2 + 1 < NT2 and mo == 1:
                    b1_pre(n2 + 1, do_stat=False)
            if n2 + 1 < NT2:
                b1_pre(n2 + 1, do_load=False)
            for sub in range(2):
                i = n2 * 2 + sub
                slot = (n2 % 2) * 2 + sub
                if sub == 1 and n2 + 1 < NT2:
                    b1_tr(n2 + 1)
                for half in range(2):
                    po_ = bank(4 + half)
                    for k in range(8):
                        P.op("pe", mm(po_, mgT[:, k, sub * 128:(sub + 1) * 128], wout[:, k, half * 512:(half + 1) * 512], k == 0, k == 7),
                             reads=[("mgT", k)] + wk_all("wout", 8), writes=[("pb", 4 + half)])
                    P.op("dve", tt(DV, x2t[sub][:, half * 512:(half + 1) * 512], po_, xs4[slot][:, half * 512:(half + 1) * 512], ALU.add),
                         reads=[("pb", 4 + half), ("xs", slot)], writes=[("x2t", sub)])
                P.dma(dma(x2_d[i * 128:(i + 1) * 128, :], x2t[sub][:]), reads=[("x2t", sub)], writes=[("x2d", i)], semkey="x2w%d" % sub)
        P.barrier()
        PHASE[0] += 1
        if PHASE[0] >= KSTOP:
            P.disabled = True

        wfg = V(R1, [8, DFF], BF16)
        wfu = V(R1 + 45056, [8, DFF], BF16)
        wfd = V(R1 + 90112, [NMF, 1024], BF16)
        W2 = R1 + 135168
        assert W2 >= R4 and W2 - R4 <= 4096
        gft = V(W2, [1024], F32)
        h2T = [V(W2 + 4096 + i * 4096, [8, TB], BF16) for i in range(2)]
        aT = V(W2 + 12288, [NMF, TB], BF16)
        sgf = [V(W2 + 23552 + i * 1024, [TB], F32) for i in range(2)]
        x3t = [V(W2 + 25600 + i * 4096, [1024], F32) for i in range(2)]
        junk = V(W2 + 33792, [1024], BF16)
        xnb[0] = [xn, V(W2 + 35840, [1024], BF16)]
        assert W2 + 37888 <= ARENA_B
        P.dma(dma(gft[:], gf_d), writes=["gft"], semkey="gf")
        for c0 in range(0, DFF, 1024):
            load_w(wfg_d, wfg, DFF, 8, "wfg", per_piece_sem=True, pieces=[(kc, c0) for kc in range(8)])
            load_w(wfu_d, wfu, DFF, 8, "wfu", per_piece_sem=True, pieces=[(kc, c0) for kc in range(8)])
        load_w(wfd_d, wfd, 1024, NMF, "wfd", per_piece_sem=True)

        def b2_pre(n2, **kw):
            for sub in range(2):
                i = n2 * 2 + sub
                norm_pre(x2_d[i * 128:(i + 1) * 128, :], (n2 % 2) * 2 + sub, sub, **kw)

        def b2_tr(n2):
            for sub in range(2):
                norm_tr(sub, h2T[n2 % 2][:, :, sub * 128:(sub + 1) * 128], 6 + sub, ("h2T", n2 % 2, sub), C("g2T"))

        b2_pre(0)
        b2_tr(0)
        for n2 in range(NT2):
            hh = h2T[n2 % 2]
            hk = [("h2T", n2 % 2, 0), ("h2T", n2 % 2, 1)]
            for mo in range(NMF):
                pg, pu_ = bank(mo % 2), bank(2 + mo % 2)
                ms = slice(mo * 128, (mo + 1) * 128)
                for k in range(8):
                    P.op("pe", mm(pg[:, 0:TB], wfg[:, k, ms], hh[:, k, :], k == 0, k == 7),
                         reads=hk + [("wfg", kc_, (mo * 128) // 1024 * 1024) for kc_ in range(8)], writes=[("pb", mo % 2)])
                for k in range(8):
                    P.op("pe", mm(pu_[:, 0:TB], wfu[:, k, ms], hh[:, k, :], k == 0, k == 7),
                         reads=hk + [("wfu", kc_, (mo * 128) // 1024 * 1024) for kc_ in range(8)], writes=[("pb", 2 + mo % 2)])
                P.op("act", act(sgf[mo % 2][:], pg[:, 0:TB], AF.Silu), reads=[("pb", mo % 2)], writes=[("sgf", mo % 2)])
                P.op("dve", tt(DV, aT[:, mo, :], pu_[:, 0:TB], sgf[mo % 2][:], ALU.mult),
                     reads=[("pb", 2 + mo % 2), ("sgf", mo % 2)], writes=[("aT", mo)])
                if n2 + 1 < NT2 and mo == 4:
                    b2_pre(n2 + 1, do_stat=False)
            if n2 + 1 < NT2:
                b2_pre(n2 + 1, do_load=False)
            for sub in range(2):
                i = n2 * 2 + sub
                slot = (n2 % 2) * 2 + sub
                if sub == 1 and n2 + 1 < NT2:
                    b2_tr(n2 + 1)
                for half in range(2):
                    po_ = bank(4 + half)
                    for mo in range(NMF):
                        P.op("pe", mm(po_, aT[:, mo, sub * 128:(sub + 1) * 128], wfd[:, mo, half * 512:(half + 1) * 512], mo == 0, mo == NMF - 1),
                             reads=[("aT", mo)] + [("wfd", m_, 0) for m_ in range((mo // 6) * 6, min(NMF, (mo // 6) * 6 + 6))], writes=[("pb", 4 + half)])
                    P.op("dve", tt(DV, x3t[sub][:, half * 512:(half + 1) * 512], po_, xs4[slot][:, half * 512:(half + 1) * 512], ALU.add),
                         reads=[("pb", 4 + half), ("xs", slot)], writes=[("x3t", sub)])
                col = sscol[0] % 96
                sscol[0] += 1
                P.op("act", act(junk[:], x3t[sub][:], AF.Square, accum=ssb[:, col:col + 1]),
                     reads=[("x3t", sub)], writes=["junk", ("ss", col)])
                P.op("act", act(rsb[:, col:col + 1], ssb[:, col:col + 1], AF.Sqrt, scale=1.0 / D, bias=EPS),
                     reads=[("ss", col)], writes=[("rs", col)])
                P.op("dve", lambda col=col: DV.reciprocal(out=rstd[:, col:col + 1], in_=rsb[:, col:col + 1]),
                     reads=[("rs", col)], writes=[("rstd", col)])
                P.op("dve", stt(x3t[sub][:], x3t[sub][:], rstd[:, col:col + 1], gft[:], ALU.mult, ALU.mult),
                     reads=[("x3t", sub), ("rstd", col), "gft"], writes=[("x3t", sub)])
                P.dma(dma(y_d[i * 128:(i + 1) * 128, :], x3t[sub][:]), reads=[("x3t", sub)], semkey="yw%d" % sub)
        stats = P.emit()
        print("PROG", stats)
    return nc


def _consts():
    cst = np.zeros((128, NCST), np.float32)

    def put(name, arr):
        o, w = CST[name]
        cst[:, o:o + w] = arr.reshape(128, w)
    put("ident", np.eye(128, dtype=np.float32))
    pos = np.arange(L, dtype=np.float32)
    inv_freq = (np.float32(500000.0) ** (-np.arange(0, 16, 2, dtype=np.float32) / np.float32(16))).astype(np.float32)
    ang = pos[:, None] * inv_freq[None, :]
    cosv = np.cos(ang).astype(np.float32).reshape(NT, 128, 8).transpose(1, 0, 2)
    sinv = np.sin(ang).astype(np.float32).reshape(NT, 128, 8).transpose(1, 0, 2)
    put("ropec", np.ascontiguousarray(cosv))
    put("ropes", np.ascontiguousarray(sinv))
    s_i = np.repeat(np.arange(8), 16)
    put("smf", (s_i[:, None] <= s_i[None, :]).astype(np.float32))
    put("smb", (s_i[:, None] >= s_i[None, :]).astype(np.float32))
    put("iota", np.broadcast_to(np.arange(512, dtype=np.float32), (128, 512)).copy())
    put("kvals", np.broadcast_to(np.arange(-7, 9, dtype=np.float32), (128, 16)).copy())
    sg = np.ones((128, 1), np.float32)
    sg[:64] = -1
    put("sgn1", sg)
    put("sgn2", -sg)
    cstb = np.zeros((128, NCSTB), np.float32)
    cstb[:, 0:128] = np.eye(128)
    kk = np.arange(128)[:, None]
    qq = np.arange(128)[None, :]
    m_prev = np.where(qq <= kk, 1.0, 0.0)
    m_next = np.where(kk <= qq, 1.0, 0.0)
    cstb[:, 128:128 + 512] = np.tile(m_prev, (1, 4))
    cstb[:, 640:640 + 512] = np.tile(m_next, (1, 4))
    return cst, cstb.astype(ml_dtypes.bfloat16)


def _jperm():
    J = np.zeros((128, 128), np.float32)
    for p in range(64):
        J[64 + p, p] = 1.0
        J[p, 64 + p] = -1.0
    return J.astype(ml_dtypes.bfloat16)


_NC_CACHE = {}


def kernel(x, norm1_g, w_in, attn_sink, ssm_lambda_re, ssm_lambda_im, ssm_log_dt,
           ssm_b_re, ssm_b_im, ssm_c_re, ssm_c_im, ssm_d, w_glu, w_attn_branch,
           w_ssm_branch, w_out, norm2_g, w_ffn_gate, w_ffn_up, w_ffn_down, norm_f_g):
    f = lambda a: np.ascontiguousarray(np.asarray(a, dtype=np.float32))
    x = f(x)
    cst, cstb = _consts()

    def put(name, arr):
        o, w = CST[name]
        cst[:, o:o + w] = np.asarray(arr, np.float32).reshape(128, w)
    put("g1T", f(norm1_g)[0].reshape(8, 128).T)
    put("g2T", f(norm2_g)[0].reshape(8, 128).T)
    put("sink", np.broadcast_to(f(attn_sink)[0], (128, 8)))
    put("dcol", np.tile(f(ssm_d)[0].T, (8, 1)))
    lr = f(ssm_lambda_re)[0].reshape(64, 64).T
    li = f(ssm_lambda_im)[0].reshape(64, 64).T
    put("lamr", np.concatenate([lr, lr], 0))
    put("lami", np.concatenate([li, li], 0))
    put("ldt", np.broadcast_to(f(ssm_log_dt)[0].reshape(1, 64), (128, 64)))
    brT = f(ssm_b_re)[0].transpose(2, 0, 1, 3).reshape(64, 1024)
    biT = f(ssm_b_im)[0].transpose(2, 0, 1, 3).reshape(64, 1024)
    crT = f(ssm_c_re)[0].transpose(2, 0, 1).reshape(64, 512)
    ciT = f(ssm_c_im)[0].transpose(2, 0, 1).reshape(64, 512)
    shared = {
        "w_in": f(w_in)[0], "w_glu": f(w_glu)[0], "w_ab": f(w_attn_branch)[0], "w_sb": f(w_ssm_branch)[0],
        "w_out": f(w_out)[0], "w_fg": f(w_ffn_gate)[0], "w_fu": f(w_ffn_up)[0], "w_fd": f(w_ffn_down)[0],
        "gfb": np.ascontiguousarray(np.broadcast_to(f(norm_f_g).reshape(1, D), (128, D))),
        "cst": cst, "cstb": cstb,
        "B1": np.ascontiguousarray(np.concatenate([brT, biT], 0)),
        "B2": np.ascontiguousarray(np.concatenate([biT, brT], 0)),
        "CT1": np.ascontiguousarray(np.concatenate([crT, ciT], 0)),
        "CT2": np.ascontiguousarray(np.concatenate([ciT, crT], 0)),
        "jperm": _jperm(),
    }
    if "nc" not in _NC_CACHE:
        _NC_CACHE["nc"] = build()
    nc = _NC_CACHE["nc"]
    in_maps = [dict(shared, x=x[b]) for b in range(8)]
    res = run_bass_kernel_spmd(nc, in_maps, core_ids=list(range(8)))
    return np.stack([np.asarray(r["y"], dtype=np.float32) for r in res.results], axis=0)
```
